# Optimizing a Trainium2 kernel written in Bass

```python
import jax, jax.numpy as jnp
from jax import lax
import numpy as np

D_MODEL = 1024
BATCH = 2
SEQ = 8192
DEPTH = 4

N_MIXERS = 3
HEAD_DIM = 64
RMS_EPS = 1e-6
SWA_Q_HEADS = D_MODEL // HEAD_DIM
SWA_KV_HEADS = SWA_Q_HEADS // 4
SWA_WINDOW = 128
ROPE_THETA = 150000.0
GLA_HEADS = 4
GLA_DK = D_MODEL // 2
GLA_DV = D_MODEL
GLA_RANK = 16
GLA_TAU = 16.0
GLA_CHUNK = 64
FOX_HEADS = D_MODEL // HEAD_DIM
FOX_BLOCK = 128
D_FF = -(-8 * D_MODEL // (3 * 256)) * 256

kernel_name = "hybrid_swa_gla_fox_adaln_trunk"


def rmsnorm(x, g):
    xf = x.astype(jnp.float32)
    y = xf * lax.rsqrt(jnp.mean(xf * xf, axis=-1, keepdims=True) + RMS_EPS)
    return (y * g.astype(jnp.float32)).astype(x.dtype)


def modulate(h, shift, scale):
    return h * (1 + scale[:, None, :]) + shift[:, None, :]


def rope(x, pos):
    hd = x.shape[-1]
    inv = 1.0 / (ROPE_THETA ** (jnp.arange(0, hd, 2, dtype=jnp.float32) / hd))
    ang = pos[:, None] * inv[None, :]
    cos = jnp.cos(ang)[None, :, None, :]
    sin = jnp.sin(ang)[None, :, None, :]
    xf = x.astype(jnp.float32)
    x1, x2 = jnp.split(xf, 2, axis=-1)
    return jnp.concatenate([x1 * cos - x2 * sin, x2 * cos + x1 * sin], axis=-1).astype(x.dtype)


def swa_mixer(h, w_in, sinks, w_o):
    B, S, _ = h.shape
    W, hd, Hq, Hkv = SWA_WINDOW, HEAD_DIM, SWA_Q_HEADS, SWA_KV_HEADS
    G = Hq // Hkv
    nb = S // W
    proj = h @ w_in
    q, k, v = jnp.split(proj, [Hq * hd, (Hq + Hkv) * hd], axis=-1)
    pos = jnp.arange(S, dtype=jnp.float32)
    q = rope(q.reshape(B, S, Hq, hd), pos)
    k = rope(k.reshape(B, S, Hkv, hd), pos)
    v = v.reshape(B, S, Hkv, hd)
    qb = q.reshape(B, nb, W, Hkv, G, hd)
    kb = k.reshape(B, nb, W, Hkv, hd)
    vb = v.reshape(B, nb, W, Hkv, hd)
    pad = ((0, 0), (1, 0), (0, 0), (0, 0), (0, 0))
    kk = jnp.concatenate([jnp.pad(kb, pad)[:, :-1], kb], axis=2)
    vv = jnp.concatenate([jnp.pad(vb, pad)[:, :-1], vb], axis=2)
    scores = jnp.einsum('bnqkgd,bnskd->bnkgqs', qb, kk).astype(jnp.float32) * (hd ** -0.5)
    i = jnp.arange(W)[:, None]
    j = jnp.arange(2 * W)[None, :]
    diff = i - j + W
    band = (diff >= 0) & (diff < W)
    has_prev = (jnp.arange(nb) > 0)[:, None, None] | (j >= W)[None]
    mask = band[None] & has_prev
    scores = jnp.where(mask[None, :, None, None], scores, -jnp.inf)
    sink_col = jnp.broadcast_to(
        sinks.astype(jnp.float32).reshape(Hkv, G)[None, None, :, :, None, None],
        scores.shape[:-1] + (1,))
    probs = jax.nn.softmax(jnp.concatenate([scores, sink_col], axis=-1), axis=-1)[..., :-1]
    out = jnp.einsum('bnkgqs,bnskd->bnqkgd', probs.astype(vv.dtype), vv)
    return out.reshape(B, S, Hq * hd) @ w_o


def gla_chunk_step(state, inp):
    q, k, v, b = inp
    C = q.shape[2]
    causal = jnp.tril(jnp.ones((C, C), dtype=bool))
    diff = b[:, :, :, None, :] - b[:, :, None, :, :]
    decay = jnp.exp(jnp.where(causal[None, None, :, :, None], diff, -jnp.inf))
    attn = jnp.einsum('bhtd,bhsd,bhtsd->bhts', q, k, decay)
    o = attn @ v + jnp.einsum('bhtd,bhde->bhte', q * jnp.exp(b), state)
    b_last = b[:, :, -1:, :]
    state = jnp.exp(b_last[:, :, 0, :])[..., None] * state + \
        jnp.einsum('bhsd,bhse->bhde', k * jnp.exp(b_last - b), v)
    return state, o


def gla_mixer(h, w_in, w_gate_up, b_gate, head_norm, w_o):
    B, S, _ = h.shape
    H, C = GLA_HEADS, GLA_CHUNK
    dk, dv = GLA_DK // H, GLA_DV // H
    nc = S // C
    proj = h @ w_in
    q, k, v, r, a_low = jnp.split(
        proj, [GLA_DK, 2 * GLA_DK, 2 * GLA_DK + GLA_DV, 2 * GLA_DK + 2 * GLA_DV], axis=-1)
    log_alpha = jax.nn.log_sigmoid((a_low @ w_gate_up + b_gate).astype(jnp.float32)) / GLA_TAU

    def to_chunks(t, d):
        return t.astype(jnp.float32).reshape(B, nc, C, H, d).transpose(1, 0, 3, 2, 4)

    qc = to_chunks(q, dk) * (dk ** -0.5)
    kc = to_chunks(k, dk)
    vc = to_chunks(v, dv)
    bc = jnp.cumsum(to_chunks(log_alpha, dk), axis=3)
    state0 = jnp.zeros((B, H, dk, dv), jnp.float32)
    _, o = lax.scan(gla_chunk_step, state0, (qc, kc, vc, bc))
    o = o.transpose(1, 0, 3, 2, 4).reshape(B, S, H, dv).astype(h.dtype)
    o = rmsnorm(o, head_norm)
    o = o * jax.nn.silu(r).reshape(B, S, H, dv)
    return o.reshape(B, S, GLA_DV) @ w_o


def fox_mixer(h, w_in, b_f, w_o):
    B, S, D = h.shape
    H, hd, blk = FOX_HEADS, HEAD_DIM, FOX_BLOCK
    proj = h @ w_in
    q, k, v, f_logit = jnp.split(proj, [D, 2 * D, 3 * D], axis=-1)
    log_f = jax.nn.log_sigmoid(f_logit.astype(jnp.float32) + b_f.astype(jnp.float32))
    lc = jnp.cumsum(log_f, axis=1).transpose(0, 2, 1)
    q = q.reshape(B, S, H, hd).transpose(0, 2, 1, 3)
    k = k.reshape(B, S, H, hd).transpose(0, 2, 1, 3)
    v = v.reshape(B, S, H, hd).transpose(0, 2, 1, 3)
    outs = []
    for i in range(S // blk):
        t0, t1 = i * blk, (i + 1) * blk
        s = jnp.einsum('bhqd,bhkd->bhqk', q[:, :, t0:t1], k[:, :, :t1]).astype(jnp.float32) * (hd ** -0.5)
        s = s + lc[:, :, t0:t1, None] - lc[:, :, None, :t1]
        mask = jnp.arange(t1)[None, :] <= (t0 + jnp.arange(blk))[:, None]
        p = jax.nn.softmax(jnp.where(mask[None, None], s, -jnp.inf), axis=-1)
        outs.append(jnp.einsum('bhqk,bhkd->bhqd', p.astype(v.dtype), v[:, :, :t1]))
    o = jnp.concatenate(outs, axis=2).transpose(0, 2, 1, 3).reshape(B, S, D)
    return o @ w_o


def swiglu(h, w_gu, w_down):
    g, u = jnp.split(h @ w_gu, 2, axis=-1)
    return (jax.nn.silu(g) * u) @ w_down


def setup_inputs(seed: int = 0) -> dict:
    key = jax.random.key(seed)
    ks = iter(jax.random.split(key, 32))
    D = D_MODEL
    n_swa, n_gla, n_fox = (DEPTH + 2) // 3, (DEPTH + 1) // 3, DEPTH // 3
    nrm = lambda k, shape, s: s * jax.random.normal(k, shape, jnp.float32)
    swa_cols = (SWA_Q_HEADS + 2 * SWA_KV_HEADS) * HEAD_DIM
    gla_cols = 2 * GLA_DK + 2 * GLA_DV + GLA_RANK
    fox_cols = 3 * D + FOX_HEADS
    return {
        "x": nrm(next(ks), (BATCH, SEQ, D), 1.0),
        "c": nrm(next(ks), (BATCH, D), 1.0),
        "ada_w": nrm(next(ks), (DEPTH, D, 6 * D), 0.5 * D ** -0.5),
        "ada_b": nrm(next(ks), (DEPTH, 6 * D), 0.02),
        "norm_gain": 1.0 + nrm(next(ks), (DEPTH, 2, D), 0.05),
        "ffn_w_gu": nrm(next(ks), (DEPTH, D, 2 * D_FF), D ** -0.5),
        "ffn_w_down": nrm(next(ks), (DEPTH, D_FF, D), D_FF ** -0.5),
        "swa_w_in": nrm(next(ks), (n_swa, D, swa_cols), D ** -0.5),
        "swa_sinks": nrm(next(ks), (n_swa, SWA_Q_HEADS), 0.5),
        "swa_w_o": nrm(next(ks), (n_swa, SWA_Q_HEADS * HEAD_DIM, D), (SWA_Q_HEADS * HEAD_DIM) ** -0.5),
        "gla_w_in": nrm(next(ks), (n_gla, D, gla_cols), D ** -0.5),
        "gla_w_gate_up": nrm(next(ks), (n_gla, GLA_RANK, GLA_DK), GLA_RANK ** -0.5),
        "gla_b_gate": nrm(next(ks), (n_gla, GLA_DK), 0.1),
        "gla_head_norm": 1.0 + nrm(next(ks), (n_gla, GLA_DV // GLA_HEADS), 0.05),
        "gla_w_o": nrm(next(ks), (n_gla, GLA_DV, D), GLA_DV ** -0.5),
        "fox_w_in": nrm(next(ks), (n_fox, D, fox_cols), D ** -0.5),
        "fox_b_f": jax.random.uniform(next(ks), (n_fox, FOX_HEADS), jnp.float32, 0.0, 3.0),
        "fox_w_o": nrm(next(ks), (n_fox, D, D), D ** -0.5),
        "final_norm": 1.0 + nrm(next(ks), (D,), 0.05),
    }


def reference(x, c, ada_w, ada_b, norm_gain, ffn_w_gu, ffn_w_down,
              swa_w_in, swa_sinks, swa_w_o,
              gla_w_in, gla_w_gate_up, gla_b_gate, gla_head_norm, gla_w_o,
              fox_w_in, fox_b_f, fox_w_o, final_norm):
    c_act = jax.nn.silu(c)
    for i in range(DEPTH):
        kind, j = i % N_MIXERS, i // N_MIXERS
        mod = c_act @ ada_w[i] + ada_b[i]
        sh1, sc1, g1, sh2, sc2, g2 = jnp.split(mod, 6, axis=-1)
        h = modulate(rmsnorm(x, norm_gain[i, 0]), sh1, sc1)
        if kind == 0:
            y = swa_mixer(h, swa_w_in[j], swa_sinks[j], swa_w_o[j])
        elif kind == 1:
            y = gla_mixer(h, gla_w_in[j], gla_w_gate_up[j], gla_b_gate[j], gla_head_norm[j], gla_w_o[j])
        else:
            y = fox_mixer(h, fox_w_in[j], fox_b_f[j], fox_w_o[j])
        x = x + g1[:, None, :] * y
        h = modulate(rmsnorm(x, norm_gain[i, 1]), sh2, sc2)
        x = x + g2[:, None, :] * swiglu(h, ffn_w_gu[i], ffn_w_down[i])
    return rmsnorm(x, final_norm)
```

```python
import numpy as np
import ml_dtypes
from contextlib import ExitStack
import concourse.bass as bass
import concourse.mybir as mybir
from concourse.bass_utils import run_bass_kernel_spmd

F32 = mybir.dt.float32
BF16 = mybir.dt.bfloat16
AF = mybir.ActivationFunctionType
ALU = mybir.AluOpType
AX = mybir.AxisListType
NPBF = ml_dtypes.bfloat16

D = 1024
DFF = 2816
NFC = DFF // 128
EPS = 1e-6
SEM_LIMIT = 30000
NDS = 12


class Dep:
    __slots__ = ("w", "r")

    def __init__(self):
        self.w = []
        self.r = {}


class Sched:
    def __init__(self, nc, es):
        self.nc = nc
        self.es = es
        self.nsem = 0
        self.eng = {}
        for name, h in (("pe", nc.tensor), ("act", nc.scalar), ("dve", nc.vector),
                        ("pool", nc.gpsimd), ("sp", nc.sync)):
            self.eng[name] = dict(h=h, name=name, waited={}, sem=None, key=None, cnt=0)
            self._fresh(self.eng[name])
        self.dsems = {}
        self.di = {}
        for qn, cnt in (("sp", NDS), ("pool", 4)):
            self.dsems[qn] = []
            self.di[qn] = 0
            for i in range(cnt):
                d = dict(sem=None, key=None, tot=0)
                self._fresh_d(d)
                self.dsems[qn].append(d)

    def _newsem(self):
        self.nsem += 1
        return self.es.enter_context(self.nc.semaphore("s%d" % self.nsem))

    def _fresh(self, e):
        e["sem"] = self._newsem()
        e["key"] = "e%d" % self.nsem
        e["cnt"] = 0

    def _fresh_d(self, d):
        d["sem"] = self._newsem()
        d["key"] = "d%d" % self.nsem
        d["tot"] = 0

    def _wait(self, e, deps):
        for (sem, key, val) in deps:
            if val <= 0 or e["waited"].get(key, 0) >= val:
                continue
            e["h"].wait_ge(sem, val)
            e["waited"][key] = val

    def _collect(self, e, R, W):
        deps = []
        for d in R:
            deps.extend(d.w)
        for d in W:
            deps.extend(d.w)
            deps.extend(d.r.values())
        if e["name"] == "pe":
            deps = [t for t in deps if t[1] != e["key"]]
        return deps

    @staticmethod
    def _mark(tk, R, W, acc=False):
        for d in R:
            old = d.r.get(tk[1])
            if old is None or old[2] < tk[2]:
                d.r[tk[1]] = tk
        for d in W:
            if acc:
                d.w = d.w + [tk]
            else:
                d.w = [tk]
            d.r = {}

    def op(self, en, fn, R=(), W=(), sig=True):
        e = self.eng[en]
        if e["cnt"] >= SEM_LIMIT:
            self._fresh(e)
        self._wait(e, self._collect(e, R, W))
        ins = fn(e["h"])
        tk = (e["sem"], e["key"], e["cnt"] + 1)
        if sig:
            ins.then_inc(e["sem"], 1)
            e["cnt"] += 1
        self._mark(tk, R, W)
        return ins

    def dma(self, qn, out, in_, R=(), W=(), acc=False):
        e = self.eng[qn]
        ring = self.dsems[qn]
        ds = ring[self.di[qn] % len(ring)]
        self.di[qn] += 1
        if ds["tot"] >= SEM_LIMIT:
            self._fresh_d(ds)
        deps = self._collect(e, R, W)
        deps.append((ds["sem"], ds["key"], ds["tot"]))
        self._wait(e, deps)
        ins = e["h"].dma_start(out=out, in_=in_)
        ins.then_inc(ds["sem"], 16)
        ds["tot"] += 16
        tk = (ds["sem"], ds["key"], ds["tot"])
        self._mark(tk, R, W, acc)
        return ins

    def barrier(self):
        tks = [(e["sem"], e["key"], e["cnt"]) for e in self.eng.values()]
        tks += [(d["sem"], d["key"], d["tot"]) for ring in self.dsems.values() for d in ring]
        for e in self.eng.values():
            self._wait(e, tks)

    def finish(self, en, deps):
        e = self.eng[en]
        all_ = []
        for d in deps:
            all_.extend(d.w)
            all_.extend(d.r.values())
        self._wait(e, all_)


_NAME = [0]


class Ctx:
    def __init__(self, nc, es):
        self.nc = nc
        self.es = es
        self.S = _FUSED["S"] if _FUSED["S"] is not None else Sched(nc, es)
        self.n = 0

    def sb(self, shape, dt, name=None):
        _NAME[0] += 1
        return self.es.enter_context(self.nc.sbuf_tensor(name or ("t%d" % _NAME[0]), list(shape), dt))

    def ps(self, shape, dt, name=None):
        _NAME[0] += 1
        return self.es.enter_context(self.nc.psum_tensor(name or ("p%d" % _NAME[0]), list(shape), dt))


_FUSED = dict(nc=None, S=None, ov=None, rank=None, T=None)


def _dram(nc, name, shape, dt, kind):
    if _FUSED["ov"] is not None:
        return _FUSED["ov"][name]
    return nc.dram_tensor(name, list(shape), dt, kind=kind).ap()


def _new_nc():
    if _FUSED["nc"] is not None:
        return _FUSED["nc"]
    return bass.Bass("TRN2", target_bir_lowering=False)


def _load_hT(S, hb, hT, blk, S_len, dhb):
    if _FUSED["ov"] is not None:
        T = _FUSED["T"]
        rr, loc = (blk * 512) // T, (blk * 512) % T
        for kk in range(4):
            S.dma("sp", hb[:, 2 * kk:2 * kk + 2, :],
                  hT[kk][rr * 256:(rr + 1) * 256, loc:loc + 512].rearrange("(k2 p) s -> p k2 s", p=128), W=[dhb], acc=(kk > 0))
    else:
        S.dma("sp", hb[:], hT.rearrange("(k p) s -> p k s", p=128)[:, :, blk * 512:(blk + 1) * 512], W=[dhb])


def _oT_dst(oT, blk):
    if _FUSED["ov"] is not None:
        T = _FUSED["T"]
        q, off = (blk * 512) // T, (blk * 512) % T
        return oT[q][:, off:off + 512]
    return oT[:, blk * 512:(blk + 1) * 512]


_NAME = [0]


class Ctx:
    def __init__(self, nc, es):
        self.nc = nc
        self.es = es
        self.S = _FUSED["S"] if _FUSED["S"] is not None else Sched(nc, es)
        self.n = 0

    def sb(self, shape, dt, name=None):
        _NAME[0] += 1
        return self.es.enter_context(self.nc.sbuf_tensor(name or ("t%d" % _NAME[0]), list(shape), dt))

    def ps(self, shape, dt, name=None):
        _NAME[0] += 1
        return self.es.enter_context(self.nc.psum_tensor(name or ("p%d" % _NAME[0]), list(shape), dt))


_FUSED = dict(nc=None, S=None, ov=None, rank=None, T=None)


def _dram(nc, name, shape, dt, kind):
    if _FUSED["ov"] is not None:
        return _FUSED["ov"][name]
    return nc.dram_tensor(name, list(shape), dt, kind=kind).ap()


def _new_nc():
    if _FUSED["nc"] is not None:
        return _FUSED["nc"]
    return bass.Bass("TRN2", target_bir_lowering=False)


def _hT_blk(hT, blk, S_len):
    if _FUSED["ov"] is not None:
        T = _FUSED["T"]
        rr, loc = (blk * 512) // T, (blk * 512) % T
        return hT[rr * D:(rr + 1) * D, loc:loc + 512].rearrange("(k p) s -> p k s", p=128)
    return hT.rearrange("(k p) s -> p k s", p=128)[:, :, blk * 512:(blk + 1) * 512]


def build_A(T, has_prev, final):
    nc = _new_nc()
    NT = T // 128
    IN, OUT = "ExternalInput", "ExternalOutput"
    x_in = _dram(nc, "x_in", [T, D], F32, IN)
    c_col = _dram(nc, "c_col", [128, 8], F32, IN)
    ident_b_d = _dram(nc, "ident_b", [128, 128], BF16, IN)
    ident_f_d = _dram(nc, "ident_f", [128, 128], F32, IN)
    if has_prev:
        oT_d = _dram(nc, "oT", [D, T], BF16, IN)
        w_o_d = _dram(nc, "w_o", [D, D], F32, IN)
        w_gu_d = _dram(nc, "w_gu", [D, 2 * DFF], F32, IN)
        w_dn_d = _dram(nc, "w_dn", [DFF, D], F32, IN)
        ada_wp = _dram(nc, "ada_wp", [D, 4 * D], F32, IN)
        ada_bp = _dram(nc, "ada_bp", [1, 4 * D], F32, IN)
        gain2_d = _dram(nc, "gain2", [1, D], F32, IN)
    if not final:
        ada_wc = _dram(nc, "ada_wc", [D, 2 * D], F32, IN)
        ada_bc = _dram(nc, "ada_bc", [1, 2 * D], F32, IN)
        gain1_d = _dram(nc, "gain1", [1, D], F32, IN)
        hT_out = _dram(nc, "hT_out", [D, T], BF16, OUT)
        if has_prev:
            x_out = _dram(nc, "x_out", [T, D], F32, OUT)
    else:
        gainf_d = _dram(nc, "gainf", [1, D], F32, IN)
        out_d = _dram(nc, "out", [T, D], F32, OUT)

    with ExitStack() as es:
        C = Ctx(nc, es)
        S = C.S
        out_deps = []
        ident_b = C.sb([128, 128], BF16)
        ident_f = C.sb([128, 128], F32)
        d_const = Dep()
        S.dma("sp", ident_b[:], ident_b_d, W=[d_const])
        S.dma("sp", ident_f[:], ident_f_d, W=[d_const], acc=True)
        ccol = C.sb([128, 8], F32)
        cact = C.sb([128, 8], F32)
        crep = C.sb([128, 8, 128], BF16)
        d_c = Dep()
        S.dma("sp", ccol[:], c_col, W=[d_c])
        S.op("act", lambda e: e.activation(out=cact[:], in_=ccol[:], func=AF.Silu), R=[d_c], W=[d_c])
        S.op("dve", lambda e: e.tensor_copy(out=crep[:], in_=cact[:].unsqueeze(2).to_broadcast([128, 8, 128])),
             R=[d_c], W=[d_c])

        pacc = [C.ps([128, 512], F32) for _ in range(4)]
        d_pacc = [Dep() for _ in range(4)]
        pgu = [C.ps([128, 512], F32) for _ in range(3)]
        d_pgu = [Dep() for _ in range(3)]
        ptr = C.ps([128, 1024], BF16)
        d_ptr = Dep()

        if has_prev:
            wo_sb = C.sb([128, 8, D], BF16)
            wdn_sb = C.sb([128, NFC, D], BF16)
            wgu_sb = C.sb([128, 8, 2 * DFF], BF16)
            G2col = C.sb([128, 8], F32)
            sh2col = C.sb([128, 8], F32)
        if not final:
            G1col = C.sb([128, 8], F32)
            sh1col = C.sb([128, 8], F32)
        else:
            gfbc = C.sb([128, D], F32)
        es_setup = ExitStack()
        C_main = C
        C = Ctx.__new__(Ctx)
        C.nc, C.es, C.S, C.n = nc, es_setup, S, 1000
        nvec = (4 if has_prev else 0) + (0 if final else 2)
        modbc = C.sb([128, max(nvec, 1), D], F32)
        d_mod = [Dep() for _ in range(max(nvec, 1))]
        slab = [C.sb([128, 8, 512], BF16) for _ in range(2)]
        d_slab = [Dep(), Dep()]
        bslab = [C.sb([128, 512], F32) for _ in range(2)]
        d_bslab = [Dep(), Dep()]
        si = 0
        srcs = []
        if has_prev:
            srcs += [(ada_wp, ada_bp, v) for v in range(4)]
        if not final:
            srcs += [(ada_wc, ada_bc, v) for v in range(2)]
        for vi, (aw, ab, v) in enumerate(srcs):
            for hf in range(2):
                c0 = v * D + hf * 512
                sl, dsl, bs, dbs = slab[si % 2], d_slab[si % 2], bslab[si % 2], d_bslab[si % 2]
                pg, dpg = pgu[si % 3], d_pgu[si % 3]
                si += 1
                S.dma("pool", sl[:], aw.rearrange("(k p) n -> p k n", p=128)[:, :, c0:c0 + 512], W=[dsl])
                S.dma("sp", bs[:], ab[:, c0:c0 + 512].partition_broadcast(128), W=[dbs])
                for k in range(8):
                    S.op("pe", lambda e, k=k: e.matmul(pg[:], crep[:, k, :], sl[:, k, :], start=(k == 0), stop=(k == 7)),
                         R=[d_c, dsl], W=[dpg], sig=(k == 7))
                S.op("dve", lambda e: e.tensor_tensor(out=modbc[:, vi, hf * 512:(hf + 1) * 512], in0=pg[:], in1=bs[:], op=ALU.add),
                     R=[dpg, dbs], W=[d_mod[vi]])

        tmpx = C.sb([128, 8, 128], F32)
        d_tmpx = Dep()
        gbc = C.sb([128, D], F32)
        d_gbc = Dep()

        def to_col(src_ap, src_deps, dst):
            S.op("dve", lambda e: e.tensor_tensor(out=tmpx[:], in0=src_ap.rearrange("p (k j) -> p k j", k=8),
                                                  in1=ident_f[:].unsqueeze(1).to_broadcast([128, 8, 128]), op=ALU.mult),
                 R=src_deps + [d_const], W=[d_tmpx])
            S.op("dve", lambda e: e.tensor_reduce(out=dst[:], in_=tmpx[:], axis=AX.X, op=ALU.add),
                 R=[d_tmpx], W=[d_cols])

        d_cols = Dep()

        def make_G(gain_d, sc_idx, dstcol):
            S.dma("sp", gbc[:], gain_d.partition_broadcast(128), W=[d_gbc])
            S.op("dve", lambda e: e.scalar_tensor_tensor(out=modbc[:, sc_idx, :], in0=modbc[:, sc_idx, :], scalar=1.0, in1=gbc[:],
                                                         op0=ALU.add, op1=ALU.mult),
                 R=[d_mod[sc_idx], d_gbc], W=[d_mod[sc_idx]])
            to_col(modbc[:, sc_idx, :], [d_mod[sc_idx]], dstcol)

        if has_prev:
            make_G(gain2_d, 2, G2col)
            to_col(modbc[:, 1, :], [d_mod[1]], sh2col)
        nb = 4 if has_prev else 0
        if not final:
            make_G(gain1_d, nb + 1, G1col)
            to_col(modbc[:, nb + 0, :], [d_mod[nb + 0]], sh1col)
        else:
            d_gf = Dep()
            S.dma("sp", gfbc[:], gainf_d.partition_broadcast(128), W=[d_gf])

        if has_prev:
            d_wo, d_wdn, d_wgu = Dep(), Dep(), Dep()
            stg = [C.sb([128, 512], F32) for _ in range(2)]
            d_stg = [Dep(), Dep()]
            wi = 0
            for k in range(8):
                for hf in range(2):
                    st, dst_ = stg[wi % 2], d_stg[wi % 2]
                    wi += 1
                    S.dma("sp", st[:], w_o_d[k * 128:(k + 1) * 128, hf * 512:(hf + 1) * 512], W=[dst_])
                    S.op("pool", lambda e, k=k, st=st, hf=hf: e.tensor_tensor(
                        out=wo_sb[:, k, hf * 512:(hf + 1) * 512], in0=st[:], in1=modbc[:, 0, hf * 512:(hf + 1) * 512], op=ALU.mult),
                        R=[dst_, d_mod[0]], W=[d_wo])
            for k in range(NFC):
                for hf in range(2):
                    st, dst_ = stg[wi % 2], d_stg[wi % 2]
                    wi += 1
                    S.dma("sp", st[:], w_dn_d[k * 128:(k + 1) * 128, hf * 512:(hf + 1) * 512], W=[dst_])
                    S.op("pool", lambda e, k=k, st=st, hf=hf: e.tensor_tensor(
                        out=wdn_sb[:, k, hf * 512:(hf + 1) * 512], in0=st[:], in1=modbc[:, 3, hf * 512:(hf + 1) * 512], op=ALU.mult),
                        R=[dst_, d_mod[3]], W=[d_wdn])
            wgu_v = w_gu_d.rearrange("(k p) n -> p k n", p=128)
            for j in range(11):
                S.dma("pool", wgu_sb[:, :, j * 512:(j + 1) * 512], wgu_v[:, :, j * 512:(j + 1) * 512], W=[d_wgu])

        S.barrier()
        es_setup.close()
        C = C_main
        NXB = 4
        xt = [C.sb([128, D], F32) for _ in range(NXB)]
        d_xt = [Dep() for _ in range(NXB)]
        xs = [C.sb([128, D], BF16) for _ in range(2)]
        d_xs = [Dep(), Dep()]
        junk = C.sb([128, D], BF16)
        d_junk = Dep()
        st4 = [C.sb([128, 4], F32) for _ in range(2)]
        d_st4 = [Dep(), Dep()]
        hT2 = [C.sb([128, 8, 256], BF16) for _ in range(2)]
        d_hT2 = [Dep(), Dep()]
        if not final:
            hTo = [C.sb([128, 8, 256], BF16) for _ in range(2)]
            d_hTo = [Dep(), Dep()]
        if has_prev:
            oTb = [C.sb([128, 8, 256], BF16) for _ in range(2)]
            d_oTb = [Dep(), Dep()]
            sg = [C.sb([128, 256], F32) for _ in range(2)]
            d_sg = [Dep(), Dep()]
            actb = [C.sb([128, 256], BF16) for _ in range(3)]
            d_actb = [Dep() for _ in range(3)]
        if final:
            ot = [C.sb([128, D], F32) for _ in range(2)]
            d_ot = [Dep(), Dep()]
        ncnt = [0]

        def emit_norm(xap, xdep, Gcol, shcol, dst, ddst, col0):
            i = ncnt[0]
            ncnt[0] += 1
            s4, ds4 = st4[i % 2], d_st4[i % 2]
            xsb, dxs = xs[i % 2], d_xs[i % 2]
            S.op("pool", lambda e: e.memset(s4[:, 0:1], 0.0), W=[ds4])
            S.op("act", lambda e: e.activation(out=junk[:], in_=xap, func=AF.Square, accum_out=s4[:, 0:1]),
                 R=[xdep], W=[d_junk, ds4])
            S.op("act", lambda e: e.activation(out=s4[:, 1:2], in_=s4[:, 0:1], func=AF.Sqrt, scale=1.0 / D, bias=EPS),
                 R=[ds4], W=[ds4])
            S.op("dve", lambda e: e.reciprocal(out=s4[:, 2:3], in_=s4[:, 1:2]), R=[ds4], W=[ds4])
            S.op("dve", lambda e: e.tensor_scalar(out=xsb[:], in0=xap, scalar1=s4[:, 2:3], scalar2=None, op0=ALU.mult),
                 R=[xdep, ds4], W=[dxs])
            for k in range(8):
                S.op("pe", lambda e, k=k: e.transpose(ptr[:, k * 128:(k + 1) * 128], xsb[:, k * 128:(k + 1) * 128], ident_b[:]),
                     R=[dxs, d_const], W=[d_ptr], sig=(k == 7))
            for k in range(8):
                if k % 2 == 0:
                    S.op("act", lambda e, k=k: e.activation(out=dst[:, k, col0:col0 + 128], in_=ptr[:, k * 128:(k + 1) * 128],
                                                             func=AF.Identity, scale=Gcol[:, k:k + 1], bias=shcol[:, k:k + 1]),
                         R=[d_ptr, d_cols], W=[ddst])
                else:
                    S.op("dve", lambda e, k=k: e.tensor_scalar(out=dst[:, k, col0:col0 + 128], in0=ptr[:, k * 128:(k + 1) * 128],
                                                                scalar1=Gcol[:, k:k + 1], scalar2=shcol[:, k:k + 1],
                                                                op0=ALU.mult, op1=ALU.add),
                         R=[d_ptr, d_cols], W=[ddst])
            return s4, ds4

        NB = T // 256
        for b in range(NB):
            tiles = [2 * b, 2 * b + 1]
            xb = [xt[t % NXB] for t in tiles]
            dxb = [d_xt[t % NXB] for t in tiles]
            for j, t in enumerate(tiles):
                S.dma("sp", xb[j][:], x_in[t * 128:(t + 1) * 128, :], W=[dxb[j]])
            if has_prev:
                ob, dob = oTb[b % 2], d_oTb[b % 2]
                if _FUSED["ov"] is not None:
                    src_oT = oT_d.rearrange("q (k p) t -> p (q k) t", p=128)[:, bass.ds(_FUSED["rank"] * 8, 8), b * 256:(b + 1) * 256]
                else:
                    src_oT = oT_d.rearrange("(k p) t -> p k t", p=128)[:, :, b * 256:(b + 1) * 256]
                S.dma("sp", ob[:], src_oT, W=[dob])
                for j in range(2):
                    for hf in range(2):
                        pa, dpa = pacc[j * 2 + hf], d_pacc[j * 2 + hf]
                        for k in range(8):
                            S.op("pe", lambda e, k=k, j=j, hf=hf, pa=pa: e.matmul(
                                pa[:], ob[:, k, j * 128:(j + 1) * 128], wo_sb[:, k, hf * 512:(hf + 1) * 512],
                                start=(k == 0), stop=(k == 7)), R=[dob, d_wo], W=[dpa], sig=(k == 7))
                        S.op("dve", lambda e, j=j, hf=hf, pa=pa: e.tensor_tensor(
                            out=xb[j][:, hf * 512:(hf + 1) * 512], in0=xb[j][:, hf * 512:(hf + 1) * 512], in1=pa[:], op=ALU.add),
                            R=[dpa, dxb[j]], W=[dxb[j]])
                h2, dh2 = hT2[b % 2], d_hT2[b % 2]
                for j in range(2):
                    emit_norm(xb[j][:], dxb[j], G2col, sh2col, h2, dh2, j * 128)
                def up(c):
                    pg, dpg = pgu[c % 3], d_pgu[c % 3]
                    for gi in range(2):
                        c0 = gi * DFF + c * 128
                        for k in range(8):
                            S.op("pe", lambda e, k=k, gi=gi, c0=c0, pg=pg: e.matmul(
                                pg[:, gi * 256:(gi + 1) * 256], wgu_sb[:, k, c0:c0 + 128], h2[:, k, :],
                                start=(k == 0), stop=(k == 7)), R=[d_wgu, dh2], W=[dpg], sig=(k == 7 and gi == 1))
                    s_, ds_ = sg[c % 2], d_sg[c % 2]
                    a_, da_ = actb[c % 3], d_actb[c % 3]
                    S.op("act", lambda e: e.activation(out=s_[:], in_=pg[:, 0:256], func=AF.Silu), R=[dpg], W=[ds_])
                    S.op("dve", lambda e: e.tensor_tensor(out=a_[:], in0=s_[:], in1=pg[:, 256:512], op=ALU.mult),
                         R=[ds_, dpg], W=[da_])

                def down(c):
                    a_, da_ = actb[c % 3], d_actb[c % 3]
                    for j in range(2):
                        for hf in range(2):
                            pa, dpa = pacc[j * 2 + hf], d_pacc[j * 2 + hf]
                            S.op("pe", lambda e, j=j, hf=hf, pa=pa: e.matmul(
                                pa[:], a_[:, j * 128:(j + 1) * 128], wdn_sb[:, c, hf * 512:(hf + 1) * 512],
                                start=(c == 0), stop=(c == NFC - 1)), R=[da_, d_wdn], W=[dpa], sig=(c == NFC - 1 or True))

                up(0)
                for c in range(NFC):
                    if c + 1 < NFC:
                        up(c + 1)
                    down(c)
                for j in range(2):
                    for hf in range(2):
                        pa, dpa = pacc[j * 2 + hf], d_pacc[j * 2 + hf]
                        S.op("dve", lambda e, j=j, hf=hf, pa=pa: e.tensor_tensor(
                            out=xb[j][:, hf * 512:(hf + 1) * 512], in0=xb[j][:, hf * 512:(hf + 1) * 512], in1=pa[:], op=ALU.add),
                            R=[dpa, dxb[j]], W=[dxb[j]])
            if not final:
                ho, dho = hTo[b % 2], d_hTo[b % 2]
                for j in range(2):
                    emit_norm(xb[j][:], dxb[j], G1col, sh1col, ho, dho, j * 128)
                S.dma("sp", hT_out.rearrange("(k p) t -> p k t", p=128)[:, :, b * 256:(b + 1) * 256], ho[:], R=[dho])
                out_deps.append(dho)
                if has_prev:
                    for j, t in enumerate(tiles):
                        S.dma("sp", x_out[t * 128:(t + 1) * 128, :], xb[j][:], R=[dxb[j]])
                        out_deps.append(dxb[j])
            else:
                for j, t in enumerate(tiles):
                    i = ncnt[0]
                    ncnt[0] += 1
                    s4, ds4 = st4[i % 2], d_st4[i % 2]
                    o_, do_ = ot[i % 2], d_ot[i % 2]
                    S.op("pool", lambda e: e.memset(s4[:, 0:1], 0.0), W=[ds4])
                    S.op("act", lambda e, j=j: e.activation(out=junk[:], in_=xb[j][:], func=AF.Square, accum_out=s4[:, 0:1]),
                         R=[dxb[j]], W=[d_junk, ds4])
                    S.op("act", lambda e: e.activation(out=s4[:, 1:2], in_=s4[:, 0:1], func=AF.Sqrt, scale=1.0 / D, bias=EPS),
                         R=[ds4], W=[ds4])
                    S.op("dve", lambda e: e.reciprocal(out=s4[:, 2:3], in_=s4[:, 1:2]), R=[ds4], W=[ds4])
                    S.op("dve", lambda e, j=j: e.scalar_tensor_tensor(out=o_[:], in0=xb[j][:], scalar=s4[:, 2:3], in1=gfbc[:],
                                                                       op0=ALU.mult, op1=ALU.mult),
                         R=[dxb[j], ds4, d_gf], W=[do_])
                    S.dma("sp", out_d[t * 128:(t + 1) * 128, :], o_[:], R=[do_])
                    out_deps.append(do_)
        S.finish("sp", out_deps)
        S.barrier()
    return nc


def _b_common(nc, S_len, F, extra_in):
    IN, OUT = "ExternalInput", "ExternalOutput"
    t = dict(hT=_dram(nc, "b_hT", [D, S_len], BF16, IN), w=_dram(nc, "b_w", [D, F], F32, IN),
             ident_b=_dram(nc, "b_ident_b", [128, 128], BF16, IN), maskc=_dram(nc, "b_maskc", [128, 128], BF16, IN),
             oT=_dram(nc, "oT_loc", [256, S_len], BF16, OUT))
    for name, shape, dt in extra_in:
        t[name] = _dram(nc, "b_" + name, shape, dt, IN)
    return t


def build_swa(S_len):
    nc = _new_nc()
    NTL = S_len // 128
    t = _b_common(nc, S_len, 384, [("cos", [128, NTL, 32], F32), ("sin", [128, NTL, 32], F32),
                                    ("sink", [128, 512], F32), ("maskp", [128, 128], BF16)])
    with ExitStack() as es:
        C = Ctx(nc, es)
        S = C.S
        ident_b = C.sb([128, 128], BF16); maskc = C.sb([128, 128], BF16); maskp = C.sb([128, 128], BF16)
        cos = C.sb([128, NTL, 32], F32); sin = C.sb([128, NTL, 32], F32); esink = C.sb([128, 512], F32)
        W = C.sb([128, 8, 384], BF16)
        dcl = [Dep() for _ in range(7)]
        S.dma("sp", ident_b[:], t["ident_b"], W=[dcl[0]]); S.dma("sp", maskc[:], t["maskc"], W=[dcl[1]])
        S.dma("sp", maskp[:], t["maskp"], W=[dcl[2]]); S.dma("sp", cos[:], t["cos"], W=[dcl[3]]); S.dma("sp", sin[:], t["sin"], W=[dcl[4]])
        S.dma("sp", esink[:], t["sink"], W=[dcl[5]])
        S.dma("pool", W[:], t["w"].rearrange("(k p) f -> p k f", p=128), W=[dcl[6]])
        for e_ in S.eng.values():
            S.finish(e_["name"], dcl)
        dc = Dep()
        S.op("act", lambda e: e.activation(out=esink[:], in_=esink[:], func=AF.Exp), W=[dc])
        hblk = [C.sb([128, 8, 512], BF16) for _ in range(2)]; d_hblk = [Dep(), Dep()]
        pproj = [C.ps([128, 512], F32) for _ in range(2)]; d_pproj = [Dep(), Dep()]
        ptq = C.ps([128, 1024], BF16); d_ptq = Dep()
        pSc = C.ps([128, 512], F32); pSp = C.ps([128, 512], F32); d_pSc = Dep(); d_pSp = Dep()
        pO = [C.ps([128, 512], F32) for _ in range(2)]; d_pO = [Dep(), Dep()]
        qk32 = [C.sb([128, 5, 64], F32) for _ in range(2)]; d_qk32 = [Dep(), Dep()]
        tt_ = [C.sb([128, 5, 32], F32) for _ in range(4)]; d_tt = [Dep() for _ in range(4)]
        qkr = [C.sb([128, 8, 64], BF16) for _ in range(2)]; d_qkr = [Dep(), Dep()]
        qkT = [C.sb([128, 512], BF16) for _ in range(3)]; d_qkT = [Dep() for _ in range(3)]
        vaug = [C.sb([128, 128], BF16) for _ in range(3)]; d_vaug = [Dep() for _ in range(3)]
        ecur = [C.sb([128, 512], BF16) for _ in range(2)]; d_ecur = [Dep(), Dep()]
        eprv = [C.sb([128, 512], BF16) for _ in range(2)]; d_eprv = [Dep(), Dep()]
        pcur = [C.sb([128, 512], BF16) for _ in range(2)]; d_pcur = [Dep(), Dep()]
        pprv = [C.sb([128, 512], BF16) for _ in range(2)]; d_pprv = [Dep(), Dep()]
        den = [C.sb([128, 512], F32) for _ in range(2)]; d_den = [Dep(), Dep()]
        oTt = [C.sb([64, 4, 512], BF16) for _ in range(2)]; d_oTt = [Dep(), Dep()]
        for i in range(3):
            S.op("dve", lambda e, i=i: e.memset(vaug[i][:, 64:128], 1.0), W=[d_vaug[i]])
        for i in range(2):
            S.op("dve", lambda e, i=i: e.memset(qkr[i][:, 5:7, :], 0.0), W=[d_qkr[i]])
        outd = []
        for n in range(NTL):
            blk, tt = n // 4, n % 4
            hb, dhb = hblk[blk % 2], d_hblk[blk % 2]
            if tt == 0:
                _load_hT(S, hb, t["hT"], blk, S_len, dhb)
            pp, dpp = pproj[n % 2], d_pproj[n % 2]
            for k in range(8):
                S.op("pe", lambda e, k=k: e.matmul(pp[:, 0:384], hb[:, k, tt * 128:(tt + 1) * 128], W[:, k, :],
                                                   start=(k == 0), stop=(k == 7)), R=[dhb], W=[dpp], sig=(k == 7))
            q3, dq3 = qk32[n % 2], d_qk32[n % 2]
            va, dva = vaug[n % 3], d_vaug[n % 3]
            q3f = q3[:].rearrange("p h d -> p (h d)")
            S.op("act", lambda e: e.activation(out=q3f[:, 0:320], in_=pp[:, 0:320], func=AF.Identity), R=[dpp], W=[dq3])
            S.op("act", lambda e: e.activation(out=va[:, 0:64], in_=pp[:, 320:384], func=AF.Identity), R=[dpp], W=[dva])
            x1, x2 = q3[:, :, 0:32], q3[:, :, 32:64]
            cb = cos[:, n, :].unsqueeze(1).to_broadcast([128, 5, 32])
            sb_ = sin[:, n, :].unsqueeze(1).to_broadcast([128, 5, 32])
            qr, dqr = qkr[n % 2], d_qkr[n % 2]
            S.op("dve", lambda e: e.tensor_tensor(out=tt_[0][:], in0=x1, in1=cb, op=ALU.mult), R=[dq3], W=[d_tt[0]])
            S.op("pool", lambda e: e.tensor_tensor(out=tt_[1][:], in0=x2, in1=sb_, op=ALU.mult), R=[dq3], W=[d_tt[1]])
            S.op("dve", lambda e: e.tensor_tensor(out=qr[:, 0:5, 0:32], in0=tt_[0][:], in1=tt_[1][:], op=ALU.subtract),
                 R=[d_tt[0], d_tt[1]], W=[dqr])
            S.op("dve", lambda e: e.tensor_tensor(out=tt_[2][:], in0=x2, in1=cb, op=ALU.mult), R=[dq3], W=[d_tt[2]])
            S.op("pool", lambda e: e.tensor_tensor(out=tt_[3][:], in0=x1, in1=sb_, op=ALU.mult), R=[dq3], W=[d_tt[3]])
            S.op("dve", lambda e: e.tensor_tensor(out=qr[:, 0:5, 32:64], in0=tt_[2][:], in1=tt_[3][:], op=ALU.add),
                 R=[d_tt[2], d_tt[3]], W=[dqr])
            S.op("pool", lambda e: e.tensor_copy(out=qr[:, 7, :], in_=qr[:, 4, :]), R=[dqr], W=[dqr])
            qr2 = qr[:].rearrange("p h d -> p (h d)")
            for j in range(4):
                S.op("pe", lambda e, j=j: e.transpose(ptq[:, j * 128:(j + 1) * 128], qr2[:, j * 128:(j + 1) * 128], ident_b[:]),
                     R=[dqr], W=[d_ptq], sig=(j == 3))
            qT, dqT = qkT[n % 3], d_qkT[n % 3]
            S.op("act", lambda e: e.activation(out=qT[:], in_=ptq[:, 0:512], func=AF.Identity), R=[d_ptq], W=[dqT])
            for h in range(4):
                S.op("pe", lambda e, h=h: e.matmul(pSc[:, h * 128:(h + 1) * 128], qT[:, 256 + (h % 2) * 128:384 + (h % 2) * 128],
                                                   qT[:, (h // 2) * 128:(h // 2 + 1) * 128], start=True, stop=True),
                     R=[dqT], W=[d_pSc], sig=(h == 3))
            ec, dec_ = ecur[n % 2], d_ecur[n % 2]
            pc, dpc = pcur[n % 2], d_pcur[n % 2]
            S.op("act", lambda e: e.activation(out=ec[:], in_=pSc[:], func=AF.Exp, scale=0.125), R=[d_pSc], W=[dec_])
            S.op("dve", lambda e: e.tensor_tensor(out=pc[:].rearrange("p (h t) -> p h t", h=4), in0=ec[:].rearrange("p (h t) -> p h t", h=4),
                                                  in1=maskc[:].unsqueeze(1).to_broadcast([128, 4, 128]), op=ALU.mult),
                 R=[dec_], W=[dpc])
            po, dpo = pO[n % 2], d_pO[n % 2]
            if n > 0:
                qTp, dqTp = qkT[(n - 1) % 3], d_qkT[(n - 1) % 3]
                for h in range(4):
                    S.op("pe", lambda e, h=h: e.matmul(pSp[:, h * 128:(h + 1) * 128], qTp[:, 256 + (h % 2) * 128:384 + (h % 2) * 128],
                                                       qT[:, (h // 2) * 128:(h // 2 + 1) * 128], start=True, stop=True),
                         R=[dqT, dqTp], W=[d_pSp], sig=(h == 3))
                ep, dep_ = eprv[n % 2], d_eprv[n % 2]
                ppv, dppv = pprv[n % 2], d_pprv[n % 2]
                S.op("act", lambda e: e.activation(out=ep[:], in_=pSp[:], func=AF.Exp, scale=0.125), R=[d_pSp], W=[dep_])
                S.op("pool", lambda e: e.tensor_tensor(out=ppv[:].rearrange("p (h t) -> p h t", h=4), in0=ep[:].rearrange("p (h t) -> p h t", h=4),
                                                       in1=maskp[:].unsqueeze(1).to_broadcast([128, 4, 128]), op=ALU.mult),
                     R=[dep_], W=[dppv])
                vap, dvap = vaug[(n - 1) % 3], d_vaug[(n - 1) % 3]
                S.op("pe", lambda e: e.matmul(po[:], vap[:], ppv[:], start=True, stop=False), R=[dvap, dppv], W=[dpo], sig=False)
            S.op("pe", lambda e: e.matmul(po[:], va[:], pc[:], start=(n == 0), stop=True), R=[dva, dpc], W=[dpo])
            dn, ddn = den[n % 2], d_den[n % 2]
            ot, dot = oTt[blk % 2], d_oTt[blk % 2]
            S.op("dve", lambda e: e.tensor_tensor(out=dn[64:128, :], in0=po[64:128, :], in1=esink[64:128, :], op=ALU.add),
                 R=[dpo, dc], W=[ddn])
            S.op("dve", lambda e: e.reciprocal(out=dn[64:128, :], in_=dn[64:128, :]), R=[ddn], W=[ddn])
            S.op("dve", lambda e: e.tensor_tensor(
                out=ot[0:64, :, tt * 128:(tt + 1) * 128], in0=po[0:64, :].rearrange("p (h t) -> p h t", h=4),
                in1=dn[64:128, :].rearrange("p (h t) -> p h t", h=4), op=ALU.mult), R=[dpo, ddn], W=[dot])
            if tt == 3:
                S.dma("sp", _oT_dst(t["oT"], blk).rearrange("(h d) s -> d h s", d=64), ot[:], R=[dot])
                outd.append(dot)
        S.finish("sp", outd)
        S.barrier()
    return nc


GLA_STOP = 99


def build_gla(S_len):
    nc = _new_nc()
    NTL = S_len // 128
    t = _b_common(nc, S_len, 784, [("wg", [128, 128], F32), ("hn", [1, 256], F32),
                                    ("U_b", [128, 128], BF16), ("ones_b", [128, 128], BF16)])
    with ExitStack() as es:
        C = Ctx(nc, es)
        S = C.S
        ident_b = C.sb([128, 128], BF16); maskc = C.sb([128, 128], BF16)
        U_b = C.sb([128, 128], BF16); ones_b = C.sb([128, 128], BF16)
        wg = C.sb([128, 128], BF16); hn = C.sb([128, 256], F32)
        W = C.sb([128, 8, 784], BF16)
        dcl = [Dep() for _ in range(7)]
        S.dma("sp", ident_b[:], t["ident_b"], W=[dcl[0]]); S.dma("sp", maskc[:], t["maskc"], W=[dcl[1]])
        S.dma("sp", U_b[:], t["U_b"], W=[dcl[2]]); S.dma("sp", ones_b[:], t["ones_b"], W=[dcl[3]])
        S.dma("pool", wg[:], t["wg"], W=[dcl[4]]); S.dma("sp", hn[:], t["hn"].partition_broadcast(128), W=[dcl[5]])
        S.dma("pool", W[:], t["w"].rearrange("(k p) f -> p k f", p=128), W=[dcl[6]])
        for e_ in S.eng.values():
            S.finish(e_["name"], dcl)
        hblk = [C.sb([128, 8, 512], BF16) for _ in range(2)]; d_hblk = [Dep(), Dep()]
        pA = C.ps([128, 512], F32); pB = C.ps([128, 512], F32); ptr = C.ps([128, 1024], BF16); pZ = C.ps([128, 512], F32)
        pC = C.ps([128, 512], F32); pO = C.ps([128, 512], F32); pKV = C.ps([128, 512], F32); ptr2 = C.ps([128, 1024], BF16)
        dpA, dpB, dptr, dpZ, dpZ2, dpC, dpO, dpKV, dptr2 = [Dep() for _ in range(9)]
        a_tm = [C.sb([128, 128], BF16) for _ in range(2)]; d_atm = [Dep(), Dep()]
        aT = C.sb([128, 128], BF16); d_aT = Dep()
        e1 = C.sb([128, 128], F32); nla = C.sb([128, 128], F32); d_e1 = Dep(); d_nla = Dep()
        nhi = C.sb([128, 128], BF16); nlo = C.sb([128, 128], BF16); nr1 = C.sb([128, 128], F32)
        d_nhi, d_nlo, d_nr1 = Dep(), Dep(), Dep()
        eb = C.sb([128, 128], F32); enb = C.sb([128, 128], F32); etot = C.sb([128, 128], F32); ebl = C.sb([128, 128], F32)
        d_eb, d_enb, d_etot, d_ebl = Dep(), Dep(), Dep(), Dep()
        dcol = C.sb([128, 1], F32); d_dcol = Dep()
        qt = C.sb([128, 128], BF16); kt = C.sb([128, 128], BF16); d_qt = Dep(); d_kt = Dep()
        qk32 = C.sb([128, 256], F32); d_qk32 = Dep()
        kdec = C.sb([128, 128], BF16); d_kdec = Dep()
        v_tm = C.sb([128, 256], BF16); d_v = Dep()
        qkT = C.sb([128, 256], BF16); d_qkT = Dep()
        attnT = C.sb([128, 128], BF16); d_attnT = Dep()
        state = C.sb([128, 256], F32); state_b = C.sb([128, 256], BF16); d_state = Dep(); d_stb = Dep()
        s4 = C.sb([128, 4], F32); d_s4 = Dep()
        on = C.sb([128, 256], F32); gate = C.sb([128, 256], F32); og = C.sb([128, 256], BF16)
        d_on, d_gate, d_og = Dep(), Dep(), Dep()
        oTt = [C.sb([128, 2, 512], BF16) for _ in range(2)]; d_oTt = [Dep(), Dep()]
        S.op("dve", lambda e: e.memset(state[:], 0.0), W=[d_state])
        S.op("dve", lambda e: e.memset(state_b[:], 0.0), W=[d_stb])
        for i in range(2):
            S.op("dve", lambda e, i=i: e.memset(a_tm[i][:], 0.0), W=[d_atm[i]])
            S.op("dve", lambda e, i=i: e.memset(a_tm[i][:, 16:17], 1.0), W=[d_atm[i]])
        outd = []
        for n in range(NTL):
            blk, tt = n // 4, n % 4
            hb, dhb = hblk[blk % 2], d_hblk[blk % 2]
            if tt == 0:
                _load_hT(S, hb, t["hT"], blk, S_len, dhb)
            for k in range(8):
                S.op("pe", lambda e, k=k: e.matmul(pA[:, 0:512], hb[:, k, tt * 128:(tt + 1) * 128], W[:, k, 0:512],
                                                   start=(k == 0), stop=(k == 7)), R=[dhb], W=[dpA], sig=(k == 7))
            for k in range(8):
                S.op("pe", lambda e, k=k: e.matmul(pB[:, 0:272], hb[:, k, tt * 128:(tt + 1) * 128], W[:, k, 512:784],
                                                   start=(k == 0), stop=(k == 7)), R=[dhb], W=[dpB], sig=(k == 7))
            at, dat = a_tm[n % 2], d_atm[n % 2]
            S.op("act", lambda e: e.activation(out=at[:, 0:16], in_=pB[:, 256:272], func=AF.Identity), R=[dpB], W=[dat])
            S.op("pe", lambda e: e.transpose(ptr[:, 0:128], at[:], ident_b[:]), R=[dat], W=[dptr])
            S.op("act", lambda e: e.activation(out=aT[:], in_=ptr[:, 0:128], func=AF.Identity), R=[dptr], W=[d_aT])
            S.op("pe", lambda e: e.matmul(pZ[:, 0:128], aT[:], wg[:], start=True, stop=True), R=[d_aT], W=[dpZ])
            if GLA_STOP == 1:
                break
            S.op("act", lambda e: e.activation(out=e1[:], in_=pZ[:, 0:128], func=AF.Exp, scale=-1.0), R=[dpZ], W=[d_e1])
            S.op("act", lambda e: e.activation(out=nla[:], in_=e1[:], func=AF.Ln, bias=1.0), R=[d_e1], W=[d_nla])
            if GLA_STOP == 2:
                break
            S.op("act", lambda e: e.activation(out=nhi[:], in_=nla[:], func=AF.Identity), R=[d_nla], W=[d_nhi])
            S.op("dve", lambda e: e.tensor_tensor(out=nr1[:], in0=nla[:], in1=nhi[:], op=ALU.subtract), R=[d_nla, d_nhi], W=[d_nr1])
            S.op("dve", lambda e: e.tensor_scalar(out=nlo[:], in0=nr1[:], scalar1=1.0, scalar2=None, op0=ALU.mult), R=[d_nr1], W=[d_nlo])
            for i_, (src_, dsrc_) in enumerate(((nhi, d_nhi), (nlo, d_nlo))):
                S.op("pe", lambda e, src_=src_, i_=i_: e.matmul(pC[:, 0:128], U_b[:], src_[:], start=(i_ == 0), stop=(i_ == 1)),
                     R=[dsrc_], W=[dpC], sig=False)
            for i_, (src_, dsrc_) in enumerate(((nhi, d_nhi), (nlo, d_nlo))):
                S.op("pe", lambda e, src_=src_, i_=i_: e.matmul(pC[:, 128:256], ones_b[:], src_[:], start=(i_ == 0), stop=(i_ == 1)),
                     R=[dsrc_], W=[dpC], sig=False)
            for i_, (src_, dsrc_) in enumerate(((nhi, d_nhi), (nlo, d_nlo))):
                S.op("pe", lambda e, src_=src_, i_=i_: e.matmul(pC[:, 256:384], src_[:], ones_b[:], start=(i_ == 0), stop=(i_ == 1)),
                     R=[dsrc_], W=[dpC], sig=(i_ == 1))
            S.op("act", lambda e: e.activation(out=eb[:], in_=pC[:, 0:128], func=AF.Exp, scale=-1.0 / 16), R=[dpC], W=[d_eb])
            S.op("act", lambda e: e.activation(out=enb[:], in_=pC[:, 0:128], func=AF.Exp, scale=1.0 / 16), R=[dpC], W=[d_enb])
            S.op("act", lambda e: e.activation(out=etot[:], in_=pC[:, 128:256], func=AF.Exp, scale=-1.0 / 16), R=[dpC], W=[d_etot])
            S.op("act", lambda e: e.activation(out=dcol[:], in_=pC[:, 256:257], func=AF.Exp, scale=-1.0 / 16), R=[dpC], W=[d_dcol])
            if GLA_STOP == 3:
                break
            S.op("pool", lambda e: e.tensor_tensor(out=ebl[:], in0=enb[:], in1=etot[:], op=ALU.mult), R=[d_enb, d_etot], W=[d_ebl])
            S.op("act", lambda e: e.activation(out=qk32[:, 0:128], in_=pA[:, 0:128], func=AF.Identity, scale=128.0 ** -0.5), R=[dpA], W=[d_qk32])
            S.op("act", lambda e: e.activation(out=qk32[:, 128:256], in_=pA[:, 128:256], func=AF.Identity), R=[dpA], W=[d_qk32])
            S.op("dve", lambda e: e.tensor_tensor(out=qt[:], in0=qk32[:, 0:128], in1=eb[:], op=ALU.mult), R=[d_qk32, d_eb], W=[d_qt])
            S.op("dve", lambda e: e.tensor_tensor(out=kt[:], in0=qk32[:, 128:256], in1=enb[:], op=ALU.mult), R=[d_qk32, d_enb], W=[d_kt])
            S.op("dve", lambda e: e.tensor_tensor(out=kdec[:], in0=qk32[:, 128:256], in1=ebl[:], op=ALU.mult), R=[d_qk32, d_ebl], W=[d_kdec])
            S.op("act", lambda e: e.activation(out=v_tm[:], in_=pA[:, 256:512], func=AF.Identity), R=[dpA], W=[d_v])
            S.op("pe", lambda e: e.transpose(ptr[:, 128:256], qt[:], ident_b[:]), R=[d_qt], W=[dptr], sig=False)
            S.op("pe", lambda e: e.transpose(ptr[:, 256:384], kt[:], ident_b[:]), R=[d_kt], W=[dptr])
            S.op("act", lambda e: e.activation(out=qkT[:], in_=ptr[:, 128:384], func=AF.Identity), R=[dptr], W=[d_qkT])
            if GLA_STOP == 4:
                break
            S.op("pe", lambda e: e.matmul(pZ[:, 128:256], qkT[:, 128:256], qkT[:, 0:128], start=True, stop=True), R=[d_qkT], W=[dpZ2])
            S.op("dve", lambda e: e.tensor_tensor(out=attnT[:], in0=maskc[:], in1=pZ[:, 128:256], op=ALU.mult), R=[dpZ2], W=[d_attnT])
            S.op("pe", lambda e: e.matmul(pO[:, 0:256], attnT[:], v_tm[:], start=True, stop=False), R=[d_attnT, d_v], W=[dpO], sig=False)
            S.op("pe", lambda e: e.matmul(pO[:, 0:256], qkT[:, 0:128], state_b[:], start=False, stop=True), R=[d_qkT, d_stb], W=[dpO])
            S.op("pe", lambda e: e.matmul(pKV[:, 0:256], kdec[:], v_tm[:], start=True, stop=True), R=[d_kdec, d_v], W=[dpKV])
            if GLA_STOP == 5:
                break
            S.op("dve", lambda e: e.scalar_tensor_tensor(out=state[:], in0=state[:], scalar=dcol[:, 0:1], in1=pKV[:, 0:256],
                                                         op0=ALU.mult, op1=ALU.add), R=[d_dcol, dpKV, d_state], W=[d_state])
            S.op("act", lambda e: e.activation(out=state_b[:], in_=state[:], func=AF.Identity), R=[d_state], W=[d_stb])
            if GLA_STOP == 6:
                break
            S.op("pool", lambda e: e.memset(s4[:, 0:1], 0.0), W=[d_s4])
            S.op("act", lambda e: e.activation(out=on[:], in_=pO[:, 0:256], func=AF.Square, accum_out=s4[:, 0:1]), R=[dpO], W=[d_on, d_s4])
            S.op("act", lambda e: e.activation(out=s4[:, 1:2], in_=s4[:, 0:1], func=AF.Sqrt, scale=1.0 / 256, bias=EPS), R=[d_s4], W=[d_s4])
            S.op("dve", lambda e: e.reciprocal(out=s4[:, 2:3], in_=s4[:, 1:2]), R=[d_s4], W=[d_s4])
            S.op("dve", lambda e: e.scalar_tensor_tensor(out=on[:], in0=pO[:, 0:256], scalar=s4[:, 2:3], in1=hn[:],
                                                         op0=ALU.mult, op1=ALU.mult), R=[dpO, d_s4], W=[d_on])
            S.op("act", lambda e: e.activation(out=gate[:], in_=pB[:, 0:256], func=AF.Silu), R=[dpB], W=[d_gate])
            S.op("pool", lambda e: e.tensor_tensor(out=og[:], in0=on[:], in1=gate[:], op=ALU.mult), R=[d_on, d_gate], W=[d_og])
            if GLA_STOP == 7:
                break
            S.op("pe", lambda e: e.transpose(ptr2[:, 0:128], og[:, 0:128], ident_b[:]), R=[d_og], W=[dptr2], sig=False)
            S.op("pe", lambda e: e.transpose(ptr2[:, 128:256], og[:, 128:256], ident_b[:]), R=[d_og], W=[dptr2])
            ot, dot = oTt[blk % 2], d_oTt[blk % 2]
            for c in range(2):
                S.op("act", lambda e, c=c: e.activation(out=ot[:, c, tt * 128:(tt + 1) * 128], in_=ptr2[:, c * 128:(c + 1) * 128],
                                                         func=AF.Identity), R=[dptr2], W=[dot])
            if tt == 3:
                S.dma("sp", _oT_dst(t["oT"], blk).rearrange("(c p) s -> p c s", p=128), ot[:], R=[dot])
                outd.append(dot)
        S.finish("sp", outd)
        S.barrier()
    return nc


FOX_STOP = 99


def build_fox(S_len):
    nc = _new_nc()
    NTL = S_len // 128
    NG = S_len // 512
    NF = NTL * 4
    t = _b_common(nc, S_len, 772, [("bfb", [128, 4], F32), ("ones_b", [128, 128], BF16)])
    with ExitStack() as es:
        C = Ctx(nc, es)
        S = C.S
        ident_b = C.sb([128, 128], BF16); maskc = C.sb([128, 128], BF16); ones_b = C.sb([128, 128], BF16)
        bfb = C.sb([128, 4], F32)
        W = C.sb([128, 8, 772], BF16)
        dcl = [Dep() for _ in range(5)]
        S.dma("sp", ident_b[:], t["ident_b"], W=[dcl[0]]); S.dma("sp", maskc[:], t["maskc"], W=[dcl[1]])
        S.dma("sp", ones_b[:], t["ones_b"], W=[dcl[2]]); S.dma("sp", bfb[:], t["bfb"], W=[dcl[3]])
        S.dma("pool", W[:], t["w"].rearrange("(k p) f -> p k f", p=128), W=[dcl[4]])
        for e_ in S.eng.values():
            S.finish(e_["name"], dcl)
        hblk = [C.sb([128, 8, 512], BF16) for _ in range(2)]; d_hblk = [Dep(), Dep()]
        pproj = [C.ps([128, 512], F32) for _ in range(2)]; d_pproj = [Dep(), Dep()]
        ptr = C.ps([128, 1024], BF16); d_ptr = Dep()
        pS = [C.ps([128, 512], F32) for _ in range(3)]; d_pS = [Dep() for _ in range(3)]
        pO = [C.ps([128, 512], F32) for _ in range(2)]; d_pO = [Dep(), Dep()]
        zb = C.sb([128, NTL, 4], F32); d_zb = Dep()
        nlf = C.sb([128, NF], F32); d_nlf = Dep()
        r1 = C.sb([128, NF], F32); d_r1 = Dep()
        r2 = C.sb([128, NF], F32); d_r2 = Dep()
        hi = C.sb([128, NF], BF16); mid = C.sb([128, NF], BF16); lo = C.sb([128, NF], BF16)
        d_hi, d_mid, d_lo = Dep(), Dep(), Dep()
        cs = C.sb([128, NTL, 4], F32); tot = C.sb([128, NTL, 4], F32); carry = C.sb([128, NTL, 4], F32)
        d_cs, d_tot, d_carry = Dep(), Dep(), Dep()
        qaug = C.sb([128, NTL, 4, 6], BF16); kaug = C.sb([128, NTL, 4, 6], BF16); d_qaug = Dep(); d_kaug = Dep()
        qT = C.sb([128, S_len], BF16); kT = C.sb([128, S_len], BF16); V = C.sb([128, NTL, 128], BF16)
        d_qT, d_kT, d_V = Dep(), Dep(), Dep()
        q_tm = [C.sb([128, 128], BF16) for _ in range(2)]; d_qtm = [Dep(), Dep()]
        k_tm = [C.sb([128, 128], BF16) for _ in range(2)]; d_ktm = [Dep(), Dep()]
        Pt = [C.sb([128, 512], BF16) for _ in range(3)]; d_Pt = [Dep() for _ in range(3)]
        den = [C.sb([128, 512], F32) for _ in range(2)]; d_den = [Dep(), Dep()]
        oTt = [C.sb([64, 512], BF16) for _ in range(2)]; d_oTt = [Dep(), Dep()]
        dclamp = C.sb([128, 128], F32); d_dclamp = Dep()

        def split3(src, dsrc, h_, dh_, m_, dm_, l_, dl_):
            S.op("act", lambda e: e.activation(out=h_, in_=src, func=AF.Identity), R=[dsrc], W=[dh_])
            S.op("dve", lambda e: e.tensor_tensor(out=r1[:], in0=src, in1=h_, op=ALU.subtract), R=[dsrc, dh_], W=[d_r1])
            S.op("dve", lambda e: e.tensor_scalar(out=m_, in0=r1[:], scalar1=1.0, scalar2=None, op0=ALU.mult), R=[d_r1], W=[dm_])
            S.op("dve", lambda e: e.tensor_tensor(out=r2[:], in0=r1[:], in1=m_, op=ALU.subtract), R=[d_r1, dm_], W=[d_r2])
            S.op("dve", lambda e: e.tensor_scalar(out=l_, in0=r2[:], scalar1=1.0, scalar2=None, op0=ALU.mult), R=[d_r2], W=[dl_])

        for n in range(NTL):
            blk, tt = n // 4, n % 4
            hb, dhb = hblk[blk % 2], d_hblk[blk % 2]
            if tt == 0:
                _load_hT(S, hb, t["hT"], blk, S_len, dhb)
            pp, dpp = pproj[n % 2], d_pproj[n % 2]
            for k in range(8):
                S.op("pe", lambda e, k=k: e.matmul(pp[:, 0:128], hb[:, k, tt * 128:(tt + 1) * 128], W[:, k, 644:772],
                                                   start=(k == 0), stop=(k == 7)), R=[dhb], W=[dpp], sig=(k == 7))
            S.op("dve", lambda e: e.tensor_tensor(out=zb[:, n, :], in0=bfb[:], in1=pp[:, 124:128], op=ALU.add), R=[dpp], W=[d_zb])
        zbf = zb[:].rearrange("p n h -> p (n h)")
        if FOX_STOP == 1:
            S.barrier()
            return nc
        S.op("act", lambda e: e.activation(out=r1[:], in_=zbf, func=AF.Exp, scale=-1.0), R=[d_zb], W=[d_r1])
        S.op("act", lambda e: e.activation(out=nlf[:], in_=r1[:], func=AF.Ln, bias=1.0), R=[d_r1], W=[d_nlf])
        split3(nlf[:], d_nlf, hi[:], d_hi, mid[:], d_mid, lo[:], d_lo)
        for i_, (src_, dsrc_) in enumerate(((hi, d_hi), (mid, d_mid), (lo, d_lo))):
            S.op("pe", lambda e, src_=src_, i_=i_: e.matmul(pS[0][:, 0:NF], maskc[:], src_[:], start=(i_ == 0), stop=(i_ == 2)),
                 R=[dsrc_], W=[d_pS[0]], sig=(i_ == 2))
        for i_, (src_, dsrc_) in enumerate(((hi, d_hi), (mid, d_mid), (lo, d_lo))):
            S.op("pe", lambda e, src_=src_, i_=i_: e.matmul(pS[1][:, 0:NF], ones_b[:], src_[:], start=(i_ == 0), stop=(i_ == 2)),
                 R=[dsrc_], W=[d_pS[1]], sig=(i_ == 2))
        S.op("act", lambda e: e.activation(out=cs[:].rearrange("p n h -> p (n h)"), in_=pS[0][:, 0:NF], func=AF.Identity), R=[d_pS[0]], W=[d_cs])
        S.op("act", lambda e: e.activation(out=tot[:].rearrange("p n h -> p (n h)"), in_=pS[1][:, 0:NF], func=AF.Identity), R=[d_pS[1]], W=[d_tot])
        if FOX_STOP == 2:
            S.barrier()
            return nc
        S.op("dve", lambda e: e.memset(carry[:], 0.0), W=[d_carry])
        for j in range(1, NTL):
            S.op("dve", lambda e, j=j: e.tensor_tensor(out=carry[:, j, :], in0=carry[:, j - 1, :], in1=tot[:, j - 1, :], op=ALU.add),
                 R=[d_tot, d_carry], W=[d_carry])
        S.op("dve", lambda e: e.tensor_tensor(out=cs[:], in0=cs[:], in1=carry[:], op=ALU.add), R=[d_cs, d_carry], W=[d_cs])
        split3(cs[:].rearrange("p n h -> p (n h)"), d_cs, hi[:], d_hi, mid[:], d_mid, lo[:], d_lo)
        S.op("dve", lambda e: e.memset(qaug[:], 1.0), W=[d_qaug])
        S.op("dve", lambda e: e.memset(kaug[:], 1.0), W=[d_kaug])
        for i_, (src_, dsrc_) in enumerate(((hi, d_hi), (mid, d_mid), (lo, d_lo))):
            sv = src_[:].rearrange("p (n h) -> p n h", h=4)
            S.op("dve", lambda e, sv=sv, i_=i_: e.tensor_scalar(out=qaug[:, :, :, i_], in0=sv, scalar1=-1.0, scalar2=None, op0=ALU.mult),
                 R=[dsrc_], W=[d_qaug])
            S.op("pool", lambda e, sv=sv, i_=i_: e.tensor_copy(out=kaug[:, :, :, 3 + i_], in_=sv), R=[dsrc_], W=[d_kaug])
        for i in range(2):
            S.op("dve", lambda e, i=i: e.memset(q_tm[i][:], 0.0), W=[d_qtm[i]])
            S.op("dve", lambda e, i=i: e.memset(k_tm[i][:], 0.0), W=[d_ktm[i]])
        S.op("dve", lambda e: e.memset(V[:, :, 64:128], 1.0), W=[d_V])
        if FOX_STOP == 3:
            S.barrier()
            return nc
        outd = []
        gi = 0
        si = 0
        lb = NTL // 4
        for h in range(4):
            for n in range(NTL):
                blk, tt = n // 4, n % 4
                if tt == 0:
                    hb, dhb = hblk[lb % 2], d_hblk[lb % 2]
                    lb += 1
                    _load_hT(S, hb, t["hT"], blk, S_len, dhb)
                pp, dpp = pproj[n % 2], d_pproj[n % 2]
                for k in range(8):
                    S.op("pe", lambda e, k=k, hb=hb, pp=pp: e.matmul(pp[:, 0:192], hb[:, k, tt * 128:(tt + 1) * 128], W[:, k, h * 192:(h + 1) * 192],
                                                                     start=(k == 0), stop=(k == 7)), R=[dhb], W=[dpp], sig=(k == 7))
                qm, dqm = q_tm[n % 2], d_qtm[n % 2]
                km, dkm = k_tm[n % 2], d_ktm[n % 2]
                S.op("act", lambda e, qm=qm, pp=pp: e.activation(out=qm[:, 0:64], in_=pp[:, 0:64], func=AF.Identity, scale=0.125), R=[dpp], W=[dqm])
                S.op("act", lambda e, km=km, pp=pp: e.activation(out=km[:, 0:64], in_=pp[:, 64:128], func=AF.Identity), R=[dpp], W=[dkm])
                S.op("act", lambda e, pp=pp, n=n: e.activation(out=V[:, n, 0:64], in_=pp[:, 128:192], func=AF.Identity), R=[dpp], W=[d_V])
                S.op("pool", lambda e, qm=qm, n=n: e.tensor_copy(out=qm[:, 64:70], in_=qaug[:, n, h, :]), R=[d_qaug], W=[dqm])
                S.op("pool", lambda e, km=km, n=n: e.tensor_copy(out=km[:, 64:70], in_=kaug[:, n, h, :]), R=[d_kaug], W=[dkm])
                S.op("pe", lambda e, qm=qm: e.transpose(ptr[:, 0:128], qm[:], ident_b[:]), R=[dqm], W=[d_ptr], sig=False)
                S.op("pe", lambda e, km=km: e.transpose(ptr[:, 128:256], km[:], ident_b[:]), R=[dkm], W=[d_ptr])
                S.op("act", lambda e, n=n: e.activation(out=qT[:, n * 128:(n + 1) * 128], in_=ptr[:, 0:128], func=AF.Identity), R=[d_ptr], W=[d_qT])
                S.op("act", lambda e, n=n: e.activation(out=kT[:, n * 128:(n + 1) * 128], in_=ptr[:, 128:256], func=AF.Identity), R=[d_ptr], W=[d_kT])
            if FOX_STOP == 4:
                S.barrier()
                return nc
            for m in range(NG):
                po, dpo = pO[gi % 2], d_pO[gi % 2]
                nj = 4 * m + 4
                for j in range(nj):
                    r = max(0, j - 4 * m)
                    c0 = r * 128
                    ps_, dps_ = pS[si % 3], d_pS[si % 3]
                    P_, dP_ = Pt[si % 3], d_Pt[si % 3]
                    si += 1
                    S.op("pe", lambda e, ps_=ps_, j=j, c0=c0, m=m: e.matmul(ps_[:, c0:512], kT[:, j * 128:(j + 1) * 128],
                                                                            qT[:, m * 512 + c0:(m + 1) * 512], start=True, stop=True),
                         R=[d_qT, d_kT], W=[dps_])
                    if j >= 4 * m:
                        S.op("dve", lambda e, ps_=ps_, c0=c0: e.tensor_scalar(out=dclamp[:], in0=ps_[:, c0:c0 + 128], scalar1=60.0, scalar2=None,
                                                                              op0=ALU.min), R=[dps_], W=[d_dclamp])
                        S.op("act", lambda e, P_=P_, c0=c0: e.activation(out=P_[:, c0:c0 + 128], in_=dclamp[:], func=AF.Exp), R=[d_dclamp], W=[dP_])
                        if c0 + 128 < 512:
                            S.op("act", lambda e, ps_=ps_, P_=P_, c0=c0: e.activation(out=P_[:, c0 + 128:512], in_=ps_[:, c0 + 128:512], func=AF.Exp),
                                 R=[dps_], W=[dP_])
                        S.op("pool", lambda e, P_=P_, c0=c0: e.tensor_tensor(out=P_[:, c0:c0 + 128], in0=P_[:, c0:c0 + 128], in1=maskc[:], op=ALU.mult),
                             R=[dP_], W=[dP_])
                    else:
                        S.op("act", lambda e, ps_=ps_, P_=P_, c0=c0: e.activation(out=P_[:, c0:512], in_=ps_[:, c0:512], func=AF.Exp), R=[dps_], W=[dP_])
                    S.op("pe", lambda e, po=po, P_=P_, j=j, c0=c0, nj=nj: e.matmul(po[:, c0:512], V[:, j, :], P_[:, c0:512],
                                                                                 start=(j == 0), stop=(j == nj - 1), skip_group_check=True),
                         R=[d_V, dP_], W=[dpo], sig=(j == nj - 1))
                dn, ddn = den[gi % 2], d_den[gi % 2]
                ot, dot = oTt[gi % 2], d_oTt[gi % 2]
                gi += 1
                S.op("dve", lambda e, dn=dn, po=po: e.reciprocal(out=dn[64:128, :], in_=po[64:128, :]), R=[dpo], W=[ddn])
                S.op("dve", lambda e, dn=dn, po=po, ot=ot: e.tensor_tensor(out=ot[0:64, :], in0=dn[64:128, :], in1=po[0:64, :], op=ALU.mult),
                     R=[dpo, ddn], W=[dot])
                S.dma("sp", _oT_dst(t["oT"], m)[h * 64:(h + 1) * 64, :], ot[:], R=[dot])
                outd.append(dot)
                if FOX_STOP == 5:
                    S.barrier()
                    return nc
        S.finish("sp", outd)
        S.barrier()
    return nc


_CONST = {}


def consts():
    if not _CONST:
        i = np.arange(128)
        mc = (i[:, None] <= i[None, :]).astype(np.float32)
        _CONST.update(ident_b=np.eye(128, dtype=np.float32).astype(NPBF), ident_f=np.eye(128, dtype=np.float32),
                      maskc=mc.astype(NPBF), maskp=(1.0 - mc).astype(NPBF), U_f=mc, ones_f=np.ones((128, 128), np.float32),
                      ones_b=np.ones((128, 128), np.float32).astype(NPBF))
    return _CONST


def rope_tables(S_len):
    key = ("rope", S_len)
    if key not in _CONST:
        hd = 64
        inv = (1.0 / (np.float32(150000.0) ** (np.arange(0, hd, 2, dtype=np.float32) / np.float32(hd)))).astype(np.float32)
        pos = np.arange(S_len, dtype=np.float32)
        ang = (pos[:, None] * inv[None, :]).astype(np.float32)
        cos = np.cos(ang).astype(np.float32).reshape(S_len // 128, 128, 32).transpose(1, 0, 2)
        sin = np.sin(ang).astype(np.float32).reshape(S_len // 128, 128, 32).transpose(1, 0, 2)
        _CONST[key] = (np.ascontiguousarray(cos), np.ascontiguousarray(sin))
    return _CONST[key]


def prep_swa(hT, w_in, sinks, g, S_len):
    c = consts()
    cos, sin = rope_tables(S_len)
    w = np.concatenate([w_in[:, g * 256:(g + 1) * 256], w_in[:, 1024 + g * 64:1024 + (g + 1) * 64],
                        w_in[:, 1280 + g * 64:1280 + (g + 1) * 64]], axis=1)
    sk = sinks[g * 4:(g + 1) * 4]
    sink = np.ascontiguousarray(np.broadcast_to(np.repeat(sk, 128)[None, :], (128, 512))).astype(np.float32)
    return dict(b_hT=hT, b_w=np.ascontiguousarray(w), b_ident_b=c["ident_b"], b_maskc=c["maskc"], b_maskp=c["maskp"],
                b_cos=cos, b_sin=sin, b_sink=sink)


def prep_gla(hT, w_in, w_gate_up, b_gate, head_norm, g, S_len):
    c = consts()
    w = np.concatenate([w_in[:, g * 128:(g + 1) * 128], w_in[:, 512 + g * 128:512 + (g + 1) * 128],
                        w_in[:, 1024 + g * 256:1024 + (g + 1) * 256], w_in[:, 2048 + g * 256:2048 + (g + 1) * 256],
                        w_in[:, 3072:3088]], axis=1)
    wg = np.zeros((128, 128), np.float32)
    wg[0:16] = w_gate_up[:, g * 128:(g + 1) * 128]
    wg[16] = b_gate[g * 128:(g + 1) * 128]
    return dict(b_hT=hT, b_w=np.ascontiguousarray(w), b_ident_b=c["ident_b"], b_maskc=c["maskc"], b_wg=wg,
                b_hn=np.ascontiguousarray(head_norm.reshape(1, 256)).astype(np.float32), b_U_b=c["maskc"], b_ones_b=c["ones_b"])


def prep_fox(hT, w_in, b_f, g, S_len):
    c = consts()
    cols = []
    for h in range(4 * g, 4 * g + 4):
        cols += [w_in[:, h * 64:(h + 1) * 64], w_in[:, 1024 + h * 64:1024 + (h + 1) * 64], w_in[:, 2048 + h * 64:2048 + (h + 1) * 64]]
    cols.append(w_in[:, 3072 + 4 * g:3072 + 4 * g + 4])
    w = np.concatenate(cols, axis=1)
    bfb = np.ascontiguousarray(np.broadcast_to(b_f[4 * g:4 * g + 4][None, :], (128, 4))).astype(np.float32)
    return dict(b_hT=hT, b_w=np.ascontiguousarray(w), b_ident_b=c["ident_b"], b_maskc=c["maskc"], b_bfb=bfb, b_ones_b=c["ones_b"])


_PROG = {}


def _prog(key, fn):
    if key not in _PROG:
        _PROG[key] = fn()
    return _PROG[key]


def _launch(nc, in_maps):
    res = run_bass_kernel_spmd(nc, in_maps, core_ids=list(range(8)))
    return res.results


GROUPS = [[0, 1, 2, 3], [4, 5, 6, 7]]
I32 = mybir.dt.int32
_BF = {0: 384, 1: 784, 2: 772}


def build_fused(S_len, depth=4):
    nc = bass.Bass("TRN2", target_bir_lowering=False)
    T = S_len // 4
    NTL = S_len // 128
    IN = "ExternalInput"
    ext = lambda n, sh, dt: nc.dram_tensor(n, list(sh), dt, kind=IN)
    x_ext = ext("x", [T, D], F32); c_col = ext("c_col", [128, 8], F32); rk = ext("rk", [1, 1], I32)
    cn = dict(ident_b=ext("ident_b", [128, 128], BF16), ident_f=ext("ident_f", [128, 128], F32),
              maskc=ext("maskc", [128, 128], BF16), maskp=ext("maskp", [128, 128], BF16), ones_b=ext("ones_b", [128, 128], BF16),
              cos=ext("cos", [128, NTL, 32], F32), sin=ext("sin", [128, NTL, 32], F32))
    gainf = ext("gainf", [1, D], F32)
    out_ext = nc.dram_tensor("out", [T, D], F32, kind="ExternalOutput")
    L = []
    for li in range(depth):
        kind = li % 3
        d = dict(ada_w=ext("ada_w%d" % li, [D, 6 * D], F32), ada_b=ext("ada_b%d" % li, [1, 6 * D], F32),
                 gain1=ext("gain1_%d" % li, [1, D], F32), gain2=ext("gain2_%d" % li, [1, D], F32),
                 w_gu=ext("w_gu%d" % li, [D, 2 * DFF], F32), w_dn=ext("w_dn%d" % li, [DFF, D], F32), w_o=ext("w_o%d" % li, [D, D], F32),
                 bw=ext("bw%d" % li, [D, _BF[kind]], F32))
        if kind == 0:
            d["sink"] = ext("sink%d" % li, [128, 512], F32)
        elif kind == 1:
            d["wg"] = ext("wg%d" % li, [128, 128], F32); d["hn"] = ext("hn%d" % li, [1, 256], F32)
        else:
            d["bfb"] = ext("bfb%d" % li, [128, 4], F32)
        d["hT_loc"] = nc.dram_tensor("hT_loc%d" % li, [D, T], BF16)
        d["hT_all"] = nc.dram_tensor("hT_all%d" % li, [4, 4 * 256, T], BF16)
        d["oT_loc"] = nc.dram_tensor("oT_loc%d" % li, [4, 256, T], BF16)
        d["oT_all"] = nc.dram_tensor("oT_all%d" % li, [4, D, T], BF16)
        L.append(d)
    x_buf = nc.dram_tensor("x_buf", [T, D], F32)
    with ExitStack() as es:
        S = Sched(nc, es)
        rank_reg = es.enter_context(nc.sync.register("rank"))
        nc.sync.reg_load(rank_reg, rk.ap()[0:1, 0:1])
        rank = nc.sync.snap(rank_reg, min_val=0, max_val=3)
        _FUSED.update(nc=nc, S=S, rank=rank, T=T)
        try:
            for li in range(depth + 1):
                has_prev, final = li > 0, li == depth
                ov = dict(x_in=(x_ext.ap() if li <= 1 else x_buf.ap()), c_col=c_col.ap(), ident_b=cn["ident_b"].ap(), ident_f=cn["ident_f"].ap())
                if has_prev:
                    p = L[li - 1]
                    ov.update(oT=p["oT_all"].ap(), w_o=p["w_o"].ap(), w_gu=p["w_gu"].ap(), w_dn=p["w_dn"].ap(),
                              ada_wp=p["ada_w"].ap()[:, 2 * D:6 * D], ada_bp=p["ada_b"].ap()[:, 2 * D:6 * D], gain2=p["gain2"].ap())
                if not final:
                    q = L[li]
                    ov.update(ada_wc=q["ada_w"].ap()[:, 0:2 * D], ada_bc=q["ada_b"].ap()[:, 0:2 * D], gain1=q["gain1"].ap(),
                              hT_out=q["hT_loc"].ap(), x_out=x_buf.ap())
                else:
                    ov.update(gainf=gainf.ap(), out=out_ext.ap())
                _FUSED["ov"] = ov
                build_A(T, has_prev, final)
                if final:
                    break
                q = L[li]
                for kk in range(4):
                    S.op("pool", lambda e, q=q, kk=kk: e.collective_compute(
                        "AllGather", ALU.bypass, replica_groups=GROUPS,
                        ins=[q["hT_loc"].ap()[kk * 256:(kk + 1) * 256, :].opt()], outs=[q["hT_all"].ap()[kk].opt()]))
                S.barrier()
                kind = li % 3
                ov = dict(b_hT=q["hT_all"].ap(), b_w=q["bw"].ap(), b_ident_b=cn["ident_b"].ap(), b_maskc=cn["maskc"].ap(), oT_loc=q["oT_loc"].ap())
                if kind == 0:
                    ov.update(b_cos=cn["cos"].ap(), b_sin=cn["sin"].ap(), b_sink=q["sink"].ap(), b_maskp=cn["maskp"].ap())
                    _FUSED["ov"] = ov
                    build_swa(S_len)
                elif kind == 1:
                    ov.update(b_wg=q["wg"].ap(), b_hn=q["hn"].ap(), b_U_b=cn["maskc"].ap(), b_ones_b=cn["ones_b"].ap())
                    _FUSED["ov"] = ov
                    build_gla(S_len)
                else:
                    ov.update(b_bfb=q["bfb"].ap(), b_ones_b=cn["ones_b"].ap())
                    _FUSED["ov"] = ov
                    build_fox(S_len)
                for qq in range(4):
                    S.op("pool", lambda e, q=q, qq=qq: e.collective_compute(
                        "AllGather", ALU.bypass, replica_groups=GROUPS,
                        ins=[q["oT_loc"].ap()[qq].opt()], outs=[q["oT_all"].ap()[qq].opt()]))
                S.barrier()
        finally:
            _FUSED.update(nc=None, S=None, ov=None, rank=None, T=None)
    return nc


def kernel(x, c, ada_w, ada_b, norm_gain, ffn_w_gu, ffn_w_down, swa_w_in, swa_sinks, swa_w_o,
           gla_w_in, gla_w_gate_up, gla_b_gate, gla_head_norm, gla_w_o, fox_w_in, fox_b_f, fox_w_o, final_norm):
    f32 = lambda a: np.ascontiguousarray(np.asarray(a, dtype=np.float32))
    x = f32(x); c = f32(c); ada_w = f32(ada_w); ada_b = f32(ada_b); norm_gain = f32(norm_gain)
    ffn_w_gu = f32(ffn_w_gu); ffn_w_down = f32(ffn_w_down)
    B, S_len, _ = x.shape
    T = S_len // 4
    depth = ada_w.shape[0]
    cst = consts()
    cos, sin = rope_tables(S_len)
    nc = _prog(("fused", S_len, depth), lambda: build_fused(S_len, depth))
    shared = dict(ident_b=cst["ident_b"], ident_f=cst["ident_f"], maskc=cst["maskc"], maskp=cst["maskp"], ones_b=cst["ones_b"],
                  cos=cos, sin=sin, gainf=f32(final_norm)[None, :])
    for li in range(depth):
        kind, j = li % 3, li // 3
        shared.update({"ada_w%d" % li: ada_w[li], "ada_b%d" % li: np.ascontiguousarray(ada_b[li][None, :]),
                       "gain1_%d" % li: np.ascontiguousarray(norm_gain[li, 0][None, :]), "gain2_%d" % li: np.ascontiguousarray(norm_gain[li, 1][None, :]),
                       "w_gu%d" % li: ffn_w_gu[li], "w_dn%d" % li: ffn_w_down[li], "w_o%d" % li: f32((swa_w_o, gla_w_o, fox_w_o)[kind][j])})
    maps = []
    dummy = np.zeros((1024, 8), NPBF)
    for i in range(8):
        b, g = i // 4, i % 4
        m = dict(shared)
        m.update(x=np.ascontiguousarray(x[b, g * T:(g + 1) * T]), c_col=np.ascontiguousarray(c[b].reshape(8, 128).T), rk=np.array([[g]], np.int32))
        for li in range(depth):
            kind, j = li % 3, li // 3
            if kind == 0:
                pm = prep_swa(dummy, f32(swa_w_in[j]), f32(swa_sinks[j]), g, S_len)
                m.update({"bw%d" % li: pm["b_w"], "sink%d" % li: pm["b_sink"]})
            elif kind == 1:
                pm = prep_gla(dummy, f32(gla_w_in[j]), f32(gla_w_gate_up[j]), f32(gla_b_gate[j]), f32(gla_head_norm[j]), g, S_len)
                m.update({"bw%d" % li: pm["b_w"], "wg%d" % li: pm["b_wg"], "hn%d" % li: pm["b_hn"]})
            else:
                pm = prep_fox(dummy, f32(fox_w_in[j]), f32(fox_b_f[j]), g, S_len)
                m.update({"bw%d" % li: pm["b_w"], "bfb%d" % li: pm["b_bfb"]})
        maps.append(m)
    res = run_bass_kernel_spmd(nc, maps, core_ids=list(range(8))).results
    out = np.zeros((B, S_len, D), np.float32)
    for i in range(8):
        b, g = i // 4, i % 4
        out[b, g * T:(g + 1) * T] = np.asarray(res[i]["out"], dtype=np.float32)
    return out


def kernel_unfused(x, c, ada_w, ada_b, norm_gain, ffn_w_gu, ffn_w_down, swa_w_in, swa_sinks, swa_w_o,
           gla_w_in, gla_w_gate_up, gla_b_gate, gla_head_norm, gla_w_o, fox_w_in, fox_b_f, fox_w_o, final_norm):
    f32 = lambda a: np.ascontiguousarray(np.asarray(a, dtype=np.float32))
    x = f32(x); c = f32(c); ada_w = f32(ada_w); ada_b = f32(ada_b); norm_gain = f32(norm_gain)
    ffn_w_gu = f32(ffn_w_gu); ffn_w_down = f32(ffn_w_down)
    B, S_len, _ = x.shape
    T = S_len // 4
    depth = ada_w.shape[0]
    cst = consts()
    mix_wo = []
    for i in range(depth):
        kind, j = i % 3, i // 3
        mix_wo.append(f32((swa_w_o, gla_w_o, fox_w_o)[kind][j]))
    ccol = [np.ascontiguousarray(c[b].reshape(8, 128).T) for b in range(B)]
    xcur = [[np.ascontiguousarray(x[b, r * T:(r + 1) * T]) for r in range(4)] for b in range(B)]
    oT_full = None
    out = np.zeros((B, S_len, D), np.float32)
    for li in range(depth + 1):
        has_prev, final = li > 0, li == depth
        nc = _prog(("A", T, has_prev, final), lambda: build_A(T, has_prev, final))
        maps = []
        for i in range(8):
            b, r = i // 4, i % 4
            m = dict(x_in=xcur[b][r], c_col=ccol[b], ident_b=cst["ident_b"], ident_f=cst["ident_f"])
            if has_prev:
                p = li - 1
                m.update(oT=np.ascontiguousarray(oT_full[b][:, r * T:(r + 1) * T]), w_o=mix_wo[p], w_gu=f32(ffn_w_gu[p]),
                         w_dn=f32(ffn_w_down[p]), ada_wp=np.ascontiguousarray(ada_w[p][:, 2 * D:6 * D]),
                         ada_bp=np.ascontiguousarray(ada_b[p][None, 2 * D:6 * D]), gain2=np.ascontiguousarray(norm_gain[p, 1][None, :]))
            if not final:
                m.update(ada_wc=np.ascontiguousarray(ada_w[li][:, 0:2 * D]), ada_bc=np.ascontiguousarray(ada_b[li][None, 0:2 * D]),
                         gain1=np.ascontiguousarray(norm_gain[li, 0][None, :]))
            else:
                m.update(gainf=f32(final_norm)[None, :])
            maps.append(m)
        res = _launch(nc, maps)
        if final:
            for i in range(8):
                b, r = i // 4, i % 4
                out[b, r * T:(r + 1) * T] = np.asarray(res[i]["out"], dtype=np.float32)
            break
        if has_prev:
            xcur = [[np.asarray(res[b * 4 + r]["x_out"]) for r in range(4)] for b in range(B)]
        hT_full = [np.ascontiguousarray(np.concatenate([np.asarray(res[b * 4 + r]["hT_out"]) for r in range(4)], axis=1)) for b in range(B)]
        kind, j = li % 3, li // 3
        if kind == 0:
            ncb = _prog(("swa", S_len), lambda: build_swa(S_len))
            maps = [prep_swa(hT_full[i // 4], f32(swa_w_in[j]), f32(swa_sinks[j]), i % 4, S_len) for i in range(8)]
        elif kind == 1:
            ncb = _prog(("gla", S_len), lambda: build_gla(S_len))
            maps = [prep_gla(hT_full[i // 4], f32(gla_w_in[j]), f32(gla_w_gate_up[j]), f32(gla_b_gate[j]), f32(gla_head_norm[j]),
                             i % 4, S_len) for i in range(8)]
        else:
            ncb = _prog(("fox", S_len), lambda: build_fox(S_len))
            maps = [prep_fox(hT_full[i // 4], f32(fox_w_in[j]), f32(fox_b_f[j]), i % 4, S_len) for i in range(8)]
        resb = _launch(ncb, maps)
        oT_full = [np.ascontiguousarray(np.concatenate([np.asarray(resb[b * 4 + g]["oT_loc"]) for g in range(4)], axis=0)) for b in range(B)]
    return out
```

```python
import numpy as np
import ml_dtypes
from contextlib import ExitStack
import concourse.bass as bass
import concourse.mybir as mybir
from concourse.bass_utils import run_bass_kernel_spmd

F32 = mybir.dt.float32
BF16 = mybir.dt.bfloat16
AF = mybir.ActivationFunctionType
ALU = mybir.AluOpType
AX = mybir.AxisListType
NPBF = ml_dtypes.bfloat16

D = 1024
DFF = 2816
NFC = DFF // 128
EPS = 1e-6
SEM_LIMIT = 30000
NDS = 12


class Dep:
    __slots__ = ("w", "r")

    def __init__(self):
        self.w = []
        self.r = {}


class Sched:
    def __init__(self, nc, es):
        self.nc = nc
        self.es = es
        self.nsem = 0
        self.eng = {}
        for name, h in (("pe", nc.tensor), ("act", nc.scalar), ("dve", nc.vector),
                        ("pool", nc.gpsimd), ("sp", nc.sync)):
            self.eng[name] = dict(h=h, name=name, waited={}, sem=None, key=None, cnt=0)
            self._fresh(self.eng[name])
        self.dsems = {}
        self.di = {}
        for qn, cnt in (("sp", NDS), ("pool", 4)):
            self.dsems[qn] = []
            self.di[qn] = 0
            for i in range(cnt):
                d = dict(sem=None, key=None, tot=0)
                self._fresh_d(d)
                self.dsems[qn].append(d)

    def _newsem(self):
        self.nsem += 1
        return self.es.enter_context(self.nc.semaphore("s%d" % self.nsem))

    def _fresh(self, e):
        e["sem"] = self._newsem()
        e["key"] = "e%d" % self.nsem
        e["cnt"] = 0

    def _fresh_d(self, d):
        d["sem"] = self._newsem()
        d["key"] = "d%d" % self.nsem
        d["tot"] = 0

    def _wait(self, e, deps):
        for (sem, key, val) in deps:
            if val <= 0 or e["waited"].get(key, 0) >= val:
                continue
            e["h"].wait_ge(sem, val)
            e["waited"][key] = val

    def _collect(self, e, R, W):
        deps = []
        for d in R:
            deps.extend(d.w)
        for d in W:
            deps.extend(d.w)
            deps.extend(d.r.values())
        if e["name"] == "pe":
            deps = [t for t in deps if t[1] != e["key"]]
        return deps

    @staticmethod
    def _mark(tk, R, W, acc=False):
        for d in R:
            old = d.r.get(tk[1])
            if old is None or old[2] < tk[2]:
                d.r[tk[1]] = tk
        for d in W:
            if acc:
                d.w = d.w + [tk]
            else:
                d.w = [tk]
            d.r = {}

    def op(self, en, fn, R=(), W=(), sig=True):
        e = self.eng[en]
        if e["cnt"] >= SEM_LIMIT:
            self._fresh(e)
        self._wait(e, self._collect(e, R, W))
        ins = fn(e["h"])
        tk = (e["sem"], e["key"], e["cnt"] + 1)
        if sig:
            ins.then_inc(e["sem"], 1)
            e["cnt"] += 1
        self._mark(tk, R, W)
        return ins

    def dma(self, qn, out, in_, R=(), W=(), acc=False):
        e = self.eng[qn]
        ring = self.dsems[qn]
        ds = ring[self.di[qn] % len(ring)]
        self.di[qn] += 1
        if ds["tot"] >= SEM_LIMIT:
            self._fresh_d(ds)
        deps = self._collect(e, R, () if acc else W)
        deps.append((ds["sem"], ds["key"], ds["tot"]))
        self._wait(e, deps)
        ins = e["h"].dma_start(out=out, in_=in_)
        ins.then_inc(ds["sem"], 16)
        ds["tot"] += 16
        tk = (ds["sem"], ds["key"], ds["tot"])
        self._mark(tk, R, W, acc)
        return ins

    def dma_prefetch(self, out, in_, W=(), acc=False):
        if not hasattr(self, "pf"):
            self.pf = []
            self.pfi = 0
            for i in range(11):
                d = dict(sem=None, key=None, tot=0)
                self._fresh_d(d)
                self.pf.append(d)
        e = self.eng["pool"]
        ds = self.pf[self.pfi % len(self.pf)]
        self.pfi += 1
        deps = [] if acc else self._collect(e, (), W)
        deps.append((ds["sem"], ds["key"], ds["tot"]))
        self._wait(e, deps)
        ins = e["h"].dma_start(out=out, in_=in_)
        ins.then_inc(ds["sem"], 16)
        ds["tot"] += 16
        self._mark((ds["sem"], ds["key"], ds["tot"]), (), W, acc)
        return ins

    def barrier(self):
        tks = [(e["sem"], e["key"], e["cnt"]) for e in self.eng.values()]
        tks += [(d["sem"], d["key"], d["tot"]) for ring in self.dsems.values() for d in ring]
        for e in self.eng.values():
            self._wait(e, tks)

    def finish(self, en, deps):
        e = self.eng[en]
        all_ = []
        for d in deps:
            all_.extend(d.w)
            all_.extend(d.r.values())
        self._wait(e, all_)


_NAME = [0]


class Ctx:
    def __init__(self, nc, es):
        self.nc = nc
        self.es = es
        self.S = _FUSED["S"] if _FUSED["S"] is not None else Sched(nc, es)
        self.n = 0

    def sb(self, shape, dt, name=None):
        _NAME[0] += 1
        return self.es.enter_context(self.nc.sbuf_tensor(name or ("t%d" % _NAME[0]), list(shape), dt))

    def ps(self, shape, dt, name=None):
        _NAME[0] += 1
        return self.es.enter_context(self.nc.psum_tensor(name or ("p%d" % _NAME[0]), list(shape), dt))


_FUSED = dict(nc=None, S=None, ov=None, rank=None, T=None, wgu=None)


def _dram(nc, name, shape, dt, kind):
    if _FUSED["ov"] is not None:
        return _FUSED["ov"][name]
    return nc.dram_tensor(name, list(shape), dt, kind=kind).ap()


def _new_nc():
    if _FUSED["nc"] is not None:
        return _FUSED["nc"]
    return bass.Bass("TRN2", target_bir_lowering=False)


def _load_hT(S, hb, hT, blk, S_len, dhb):
    if _FUSED["ov"] is not None:
        T = _FUSED["T"]
        rr, loc = (blk * 512) // T, (blk * 512) % T
        for kk in range(4):
            S.dma("sp", hb[:, 2 * kk:2 * kk + 2, :],
                  hT[kk][rr * 256:(rr + 1) * 256, loc:loc + 512].rearrange("(k2 p) s -> p k2 s", p=128), W=[dhb], acc=(kk > 0))
    else:
        S.dma("sp", hb[:], hT.rearrange("(k p) s -> p k s", p=128)[:, :, blk * 512:(blk + 1) * 512], W=[dhb])


def _oT_dst(oT, blk):
    if _FUSED["ov"] is not None:
        T = _FUSED["T"]
        q, off = (blk * 512) // T, (blk * 512) % T
        return oT[q][:, off:off + 512]
    return oT[:, blk * 512:(blk + 1) * 512]


_NAME = [0]


class Ctx:
    def __init__(self, nc, es):
        self.nc = nc
        self.es = es
        self.S = _FUSED["S"] if _FUSED["S"] is not None else Sched(nc, es)
        self.n = 0

    def sb(self, shape, dt, name=None):
        _NAME[0] += 1
        return self.es.enter_context(self.nc.sbuf_tensor(name or ("t%d" % _NAME[0]), list(shape), dt))

    def ps(self, shape, dt, name=None):
        _NAME[0] += 1
        return self.es.enter_context(self.nc.psum_tensor(name or ("p%d" % _NAME[0]), list(shape), dt))


_FUSED = dict(nc=None, S=None, ov=None, rank=None, T=None, wgu=None)


def _dram(nc, name, shape, dt, kind):
    if _FUSED["ov"] is not None:
        return _FUSED["ov"][name]
    return nc.dram_tensor(name, list(shape), dt, kind=kind).ap()


def _new_nc():
    if _FUSED["nc"] is not None:
        return _FUSED["nc"]
    return bass.Bass("TRN2", target_bir_lowering=False)


def _hT_blk(hT, blk, S_len):
    if _FUSED["ov"] is not None:
        T = _FUSED["T"]
        rr, loc = (blk * 512) // T, (blk * 512) % T
        return hT[rr * D:(rr + 1) * D, loc:loc + 512].rearrange("(k p) s -> p k s", p=128)
    return hT.rearrange("(k p) s -> p k s", p=128)[:, :, blk * 512:(blk + 1) * 512]


def build_A(T, has_prev, final):
    nc = _new_nc()
    NT = T // 128
    IN, OUT = "ExternalInput", "ExternalOutput"
    x_in = _dram(nc, "x_in", [T, D], F32, IN)
    c_col = _dram(nc, "c_col", [128, 8], F32, IN)
    ident_b_d = _dram(nc, "ident_b", [128, 128], BF16, IN)
    ident_f_d = _dram(nc, "ident_f", [128, 128], F32, IN)
    if has_prev:
        oT_d = _dram(nc, "oT", [D, T], BF16, IN)
        w_o_d = _dram(nc, "w_o", [D, D], F32, IN)
        w_gu_d = _dram(nc, "w_gu", [D, 2 * DFF], F32, IN)
        w_dn_d = _dram(nc, "w_dn", [DFF, D], F32, IN)
        ada_wp = _dram(nc, "ada_wp", [D, 4 * D], F32, IN)
        ada_bp = _dram(nc, "ada_bp", [1, 4 * D], F32, IN)
        gain2_d = _dram(nc, "gain2", [1, D], F32, IN)
    if not final:
        ada_wc = _dram(nc, "ada_wc", [D, 2 * D], F32, IN)
        ada_bc = _dram(nc, "ada_bc", [1, 2 * D], F32, IN)
        gain1_d = _dram(nc, "gain1", [1, D], F32, IN)
        hT_out = _dram(nc, "hT_out", [D, T], BF16, OUT)
        if has_prev:
            x_out = _dram(nc, "x_out", [T, D], F32, OUT)
    else:
        gainf_d = _dram(nc, "gainf", [1, D], F32, IN)
        out_d = _dram(nc, "out", [T, D], F32, OUT)

    with ExitStack() as es:
        C = Ctx(nc, es)
        S = C.S
        out_deps = []
        ident_b = C.sb([128, 128], BF16)
        ident_f = C.sb([128, 128], F32)
        d_const = Dep()
        S.dma("sp", ident_b[:], ident_b_d, W=[d_const])
        S.dma("sp", ident_f[:], ident_f_d, W=[d_const], acc=True)
        ccol = C.sb([128, 8], F32)
        cact = C.sb([128, 8], F32)
        crep = C.sb([128, 8, 128], BF16)
        d_c = Dep()
        S.dma("sp", ccol[:], c_col, W=[d_c])
        S.op("act", lambda e: e.activation(out=cact[:], in_=ccol[:], func=AF.Silu), R=[d_c], W=[d_c])
        S.op("dve", lambda e: e.tensor_copy(out=crep[:], in_=cact[:].unsqueeze(2).to_broadcast([128, 8, 128])),
             R=[d_c], W=[d_c])

        pacc = [C.ps([128, 512], F32) for _ in range(4)]
        d_pacc = [Dep() for _ in range(4)]
        pgu = [C.ps([128, 512], F32) for _ in range(3)]
        d_pgu = [Dep() for _ in range(3)]
        ptr = C.ps([128, 1024], BF16)
        d_ptr = Dep()

        if has_prev:
            wo_sb = C.sb([128, 8, D], BF16)
            wdn_sb = C.sb([128, NFC, D], BF16)
            if _FUSED.get("wgu") is not None:
                wgu_sb = _FUSED["wgu"][0]
            else:
                wgu_sb = C.sb([128, 8, 2 * DFF], BF16)
            G2col = C.sb([128, 8], F32)
            sh2col = C.sb([128, 8], F32)
        if not final:
            G1col = C.sb([128, 8], F32)
            sh1col = C.sb([128, 8], F32)
        else:
            gfbc = C.sb([128, D], F32)
        es_setup = ExitStack()
        C_main = C
        C = Ctx.__new__(Ctx)
        C.nc, C.es, C.S, C.n = nc, es_setup, S, 1000
        nvec = (4 if has_prev else 0) + (0 if final else 2)
        modbc = C.sb([128, max(nvec, 1), D], F32)
        d_mod = [Dep() for _ in range(max(nvec, 1))]
        slab = [C.sb([128, 8, 512], BF16) for _ in range(2)]
        d_slab = [Dep(), Dep()]
        bslab = [C.sb([128, 512], F32) for _ in range(2)]
        d_bslab = [Dep(), Dep()]
        si = 0
        srcs = []
        if has_prev:
            srcs += [(ada_wp, ada_bp, v) for v in range(4)]
        if not final:
            srcs += [(ada_wc, ada_bc, v) for v in range(2)]
        for vi, (aw, ab, v) in enumerate(srcs):
            for hf in range(2):
                c0 = v * D + hf * 512
                sl, dsl, bs, dbs = slab[si % 2], d_slab[si % 2], bslab[si % 2], d_bslab[si % 2]
                pg, dpg = pgu[si % 3], d_pgu[si % 3]
                si += 1
                S.dma("pool", sl[:], aw.rearrange("(k p) n -> p k n", p=128)[:, :, c0:c0 + 512], W=[dsl])
                S.dma("sp", bs[:], ab[:, c0:c0 + 512].partition_broadcast(128), W=[dbs])
                for k in range(8):
                    S.op("pe", lambda e, k=k: e.matmul(pg[:], crep[:, k, :], sl[:, k, :], start=(k == 0), stop=(k == 7)),
                         R=[d_c, dsl], W=[dpg], sig=(k == 7))
                S.op("dve", lambda e: e.tensor_tensor(out=modbc[:, vi, hf * 512:(hf + 1) * 512], in0=pg[:], in1=bs[:], op=ALU.add),
                     R=[dpg, dbs], W=[d_mod[vi]])

        tmpx = C.sb([128, 8, 128], F32)
        d_tmpx = Dep()
        gbc = C.sb([128, D], F32)
        d_gbc = Dep()

        def to_col(src_ap, src_deps, dst):
            S.op("dve", lambda e: e.tensor_tensor(out=tmpx[:], in0=src_ap.rearrange("p (k j) -> p k j", k=8),
                                                  in1=ident_f[:].unsqueeze(1).to_broadcast([128, 8, 128]), op=ALU.mult),
                 R=src_deps + [d_const], W=[d_tmpx])
            S.op("dve", lambda e: e.tensor_reduce(out=dst[:], in_=tmpx[:], axis=AX.X, op=ALU.add),
                 R=[d_tmpx], W=[d_cols])

        d_cols = Dep()

        def make_G(gain_d, sc_idx, dstcol):
            S.dma("sp", gbc[:], gain_d.partition_broadcast(128), W=[d_gbc])
            S.op("dve", lambda e: e.scalar_tensor_tensor(out=modbc[:, sc_idx, :], in0=modbc[:, sc_idx, :], scalar=1.0, in1=gbc[:],
                                                         op0=ALU.add, op1=ALU.mult),
                 R=[d_mod[sc_idx], d_gbc], W=[d_mod[sc_idx]])
            to_col(modbc[:, sc_idx, :], [d_mod[sc_idx]], dstcol)

        if has_prev:
            make_G(gain2_d, 2, G2col)
            to_col(modbc[:, 1, :], [d_mod[1]], sh2col)
        nb = 4 if has_prev else 0
        if not final:
            make_G(gain1_d, nb + 1, G1col)
            to_col(modbc[:, nb + 0, :], [d_mod[nb + 0]], sh1col)
        else:
            d_gf = Dep()
            S.dma("sp", gfbc[:], gainf_d.partition_broadcast(128), W=[d_gf])

        if has_prev:
            d_wo, d_wdn, d_wgu = Dep(), Dep(), Dep()
            stg = [C.sb([128, 512], F32) for _ in range(2)]
            d_stg = [Dep(), Dep()]
            wi = 0
            for k in range(8):
                for hf in range(2):
                    st, dst_ = stg[wi % 2], d_stg[wi % 2]
                    wi += 1
                    S.dma("sp", st[:], w_o_d[k * 128:(k + 1) * 128, hf * 512:(hf + 1) * 512], W=[dst_])
                    S.op("pool", lambda e, k=k, st=st, hf=hf: e.tensor_tensor(
                        out=wo_sb[:, k, hf * 512:(hf + 1) * 512], in0=st[:], in1=modbc[:, 0, hf * 512:(hf + 1) * 512], op=ALU.mult),
                        R=[dst_, d_mod[0]], W=[d_wo])
            for k in range(NFC):
                for hf in range(2):
                    st, dst_ = stg[wi % 2], d_stg[wi % 2]
                    wi += 1
                    S.dma("sp", st[:], w_dn_d[k * 128:(k + 1) * 128, hf * 512:(hf + 1) * 512], W=[dst_])
                    S.op("pool", lambda e, k=k, st=st, hf=hf: e.tensor_tensor(
                        out=wdn_sb[:, k, hf * 512:(hf + 1) * 512], in0=st[:], in1=modbc[:, 3, hf * 512:(hf + 1) * 512], op=ALU.mult),
                        R=[dst_, d_mod[3]], W=[d_wdn])
            wgu_v = w_gu_d.rearrange("(k p) n -> p k n", p=128)
            if _FUSED.get("wgu") is not None:
                d_wgu = _FUSED["wgu"][1]
            else:
                for j in range(11):
                    S.dma("pool", wgu_sb[:, :, j * 512:(j + 1) * 512], wgu_v[:, :, j * 512:(j + 1) * 512], W=[d_wgu], acc=(j > 0))

        S.barrier()
        es_setup.close()
        C = C_main
        NXB = 4
        xt = [C.sb([128, D], F32) for _ in range(NXB)]
        d_xt = [Dep() for _ in range(NXB)]
        xs = [C.sb([128, D], BF16) for _ in range(2)]
        d_xs = [Dep(), Dep()]
        junk = C.sb([128, D], BF16)
        d_junk = Dep()
        st4 = [C.sb([128, 4], F32) for _ in range(2)]
        d_st4 = [Dep(), Dep()]
        hT2 = [C.sb([128, 8, 256], BF16) for _ in range(2)]
        d_hT2 = [Dep(), Dep()]
        if not final:
            hTo = [C.sb([128, 8, 256], BF16) for _ in range(2)]
            d_hTo = [Dep(), Dep()]
        if has_prev:
            oTb = [C.sb([128, 8, 256], BF16) for _ in range(2)]
            d_oTb = [Dep(), Dep()]
            sg = [C.sb([128, 256], F32) for _ in range(2)]
            d_sg = [Dep(), Dep()]
            actb = [C.sb([128, 256], BF16) for _ in range(3)]
            d_actb = [Dep() for _ in range(3)]
        if final:
            ot = [C.sb([128, D], F32) for _ in range(2)]
            d_ot = [Dep(), Dep()]
        ncnt = [0]

        def emit_norm(xap, xdep, Gcol, shcol, dst, ddst, col0):
            i = ncnt[0]
            ncnt[0] += 1
            s4, ds4 = st4[i % 2], d_st4[i % 2]
            xsb, dxs = xs[i % 2], d_xs[i % 2]
            S.op("pool", lambda e: e.memset(s4[:, 0:1], 0.0), W=[ds4])
            S.op("act", lambda e: e.activation(out=junk[:], in_=xap, func=AF.Square, accum_out=s4[:, 0:1]),
                 R=[xdep], W=[d_junk, ds4])
            S.op("act", lambda e: e.activation(out=s4[:, 1:2], in_=s4[:, 0:1], func=AF.Sqrt, scale=1.0 / D, bias=EPS),
                 R=[ds4], W=[ds4])
            S.op("dve", lambda e: e.reciprocal(out=s4[:, 2:3], in_=s4[:, 1:2]), R=[ds4], W=[ds4])
            S.op("dve", lambda e: e.tensor_scalar(out=xsb[:], in0=xap, scalar1=s4[:, 2:3], scalar2=None, op0=ALU.mult),
                 R=[xdep, ds4], W=[dxs])
            for k in range(8):
                S.op("pe", lambda e, k=k: e.transpose(ptr[:, k * 128:(k + 1) * 128], xsb[:, k * 128:(k + 1) * 128], ident_b[:]),
                     R=[dxs, d_const], W=[d_ptr], sig=(k == 7))
            for k in range(8):
                if k % 2 == 0:
                    S.op("act", lambda e, k=k: e.activation(out=dst[:, k, col0:col0 + 128], in_=ptr[:, k * 128:(k + 1) * 128],
                                                             func=AF.Identity, scale=Gcol[:, k:k + 1], bias=shcol[:, k:k + 1]),
                         R=[d_ptr, d_cols], W=[ddst])
                else:
                    S.op("dve", lambda e, k=k: e.tensor_scalar(out=dst[:, k, col0:col0 + 128], in0=ptr[:, k * 128:(k + 1) * 128],
                                                                scalar1=Gcol[:, k:k + 1], scalar2=shcol[:, k:k + 1],
                                                                op0=ALU.mult, op1=ALU.add),
                         R=[d_ptr, d_cols], W=[ddst])
            return s4, ds4

        NB = T // 256
        for b in range(NB):
            tiles = [2 * b, 2 * b + 1]
            xb = [xt[t % NXB] for t in tiles]
            dxb = [d_xt[t % NXB] for t in tiles]
            for j, t in enumerate(tiles):
                S.dma("sp", xb[j][:], x_in[t * 128:(t + 1) * 128, :], W=[dxb[j]])
            if has_prev:
                ob, dob = oTb[b % 2], d_oTb[b % 2]
                if _FUSED["ov"] is not None:
                    src_oT = oT_d.rearrange("q (k p) t -> p (q k) t", p=128)[:, bass.ds(_FUSED["rank"] * 8, 8), b * 256:(b + 1) * 256]
                else:
                    src_oT = oT_d.rearrange("(k p) t -> p k t", p=128)[:, :, b * 256:(b + 1) * 256]
                S.dma("sp", ob[:], src_oT, W=[dob])
                for j in range(2):
                    for hf in range(2):
                        pa, dpa = pacc[j * 2 + hf], d_pacc[j * 2 + hf]
                        for k in range(8):
                            S.op("pe", lambda e, k=k, j=j, hf=hf, pa=pa: e.matmul(
                                pa[:], ob[:, k, j * 128:(j + 1) * 128], wo_sb[:, k, hf * 512:(hf + 1) * 512],
                                start=(k == 0), stop=(k == 7)), R=[dob, d_wo], W=[dpa], sig=(k == 7))
                        S.op("dve", lambda e, j=j, hf=hf, pa=pa: e.tensor_tensor(
                            out=xb[j][:, hf * 512:(hf + 1) * 512], in0=xb[j][:, hf * 512:(hf + 1) * 512], in1=pa[:], op=ALU.add),
                            R=[dpa, dxb[j]], W=[dxb[j]])
                h2, dh2 = hT2[b % 2], d_hT2[b % 2]
                for j in range(2):
                    emit_norm(xb[j][:], dxb[j], G2col, sh2col, h2, dh2, j * 128)
                def up(c):
                    pg, dpg = pgu[c % 3], d_pgu[c % 3]
                    for gi in range(2):
                        c0 = gi * DFF + c * 128
                        for k in range(8):
                            S.op("pe", lambda e, k=k, gi=gi, c0=c0, pg=pg: e.matmul(
                                pg[:, gi * 256:(gi + 1) * 256], wgu_sb[:, k, c0:c0 + 128], h2[:, k, :],
                                start=(k == 0), stop=(k == 7)), R=[d_wgu, dh2], W=[dpg], sig=(k == 7 and gi == 1))
                    s_, ds_ = sg[c % 2], d_sg[c % 2]
                    a_, da_ = actb[c % 3], d_actb[c % 3]
                    S.op("act", lambda e: e.activation(out=s_[:], in_=pg[:, 0:256], func=AF.Silu), R=[dpg], W=[ds_])
                    S.op("dve", lambda e: e.tensor_tensor(out=a_[:], in0=s_[:], in1=pg[:, 256:512], op=ALU.mult),
                         R=[ds_, dpg], W=[da_])

                def down(c):
                    a_, da_ = actb[c % 3], d_actb[c % 3]
                    for j in range(2):
                        for hf in range(2):
                            pa, dpa = pacc[j * 2 + hf], d_pacc[j * 2 + hf]
                            S.op("pe", lambda e, j=j, hf=hf, pa=pa: e.matmul(
                                pa[:], a_[:, j * 128:(j + 1) * 128], wdn_sb[:, c, hf * 512:(hf + 1) * 512],
                                start=(c == 0), stop=(c == NFC - 1)), R=[da_, d_wdn], W=[dpa], sig=(c == NFC - 1 or True))

                up(0)
                for c in range(NFC):
                    if c + 1 < NFC:
                        up(c + 1)
                    down(c)
                for j in range(2):
                    for hf in range(2):
                        pa, dpa = pacc[j * 2 + hf], d_pacc[j * 2 + hf]
                        S.op("dve", lambda e, j=j, hf=hf, pa=pa: e.tensor_tensor(
                            out=xb[j][:, hf * 512:(hf + 1) * 512], in0=xb[j][:, hf * 512:(hf + 1) * 512], in1=pa[:], op=ALU.add),
                            R=[dpa, dxb[j]], W=[dxb[j]])
            if not final:
                ho, dho = hTo[b % 2], d_hTo[b % 2]
                for j in range(2):
                    emit_norm(xb[j][:], dxb[j], G1col, sh1col, ho, dho, j * 128)
                S.dma("sp", hT_out.rearrange("(k p) t -> p k t", p=128)[:, :, b * 256:(b + 1) * 256], ho[:], R=[dho])
                out_deps.append(dho)
                if has_prev:
                    for j, t in enumerate(tiles):
                        S.dma("sp", x_out[t * 128:(t + 1) * 128, :], xb[j][:], R=[dxb[j]])
                        out_deps.append(dxb[j])
            else:
                for j, t in enumerate(tiles):
                    i = ncnt[0]
                    ncnt[0] += 1
                    s4, ds4 = st4[i % 2], d_st4[i % 2]
                    o_, do_ = ot[i % 2], d_ot[i % 2]
                    S.op("pool", lambda e: e.memset(s4[:, 0:1], 0.0), W=[ds4])
                    S.op("act", lambda e, j=j: e.activation(out=junk[:], in_=xb[j][:], func=AF.Square, accum_out=s4[:, 0:1]),
                         R=[dxb[j]], W=[d_junk, ds4])
                    S.op("act", lambda e: e.activation(out=s4[:, 1:2], in_=s4[:, 0:1], func=AF.Sqrt, scale=1.0 / D, bias=EPS),
                         R=[ds4], W=[ds4])
                    S.op("dve", lambda e: e.reciprocal(out=s4[:, 2:3], in_=s4[:, 1:2]), R=[ds4], W=[ds4])
                    S.op("dve", lambda e, j=j: e.scalar_tensor_tensor(out=o_[:], in0=xb[j][:], scalar=s4[:, 2:3], in1=gfbc[:],
                                                                       op0=ALU.mult, op1=ALU.mult),
                         R=[dxb[j], ds4, d_gf], W=[do_])
                    S.dma("sp", out_d[t * 128:(t + 1) * 128, :], o_[:], R=[do_])
                    out_deps.append(do_)
        S.finish("sp", out_deps)
        S.barrier()
    return nc


def _b_common(nc, S_len, F, extra_in):
    IN, OUT = "ExternalInput", "ExternalOutput"
    t = dict(hT=_dram(nc, "b_hT", [D, S_len], BF16, IN), w=_dram(nc, "b_w", [D, F], F32, IN),
             ident_b=_dram(nc, "b_ident_b", [128, 128], BF16, IN), maskc=_dram(nc, "b_maskc", [128, 128], BF16, IN),
             oT=_dram(nc, "oT_loc", [256, S_len], BF16, OUT))
    for name, shape, dt in extra_in:
        t[name] = _dram(nc, "b_" + name, shape, dt, IN)
    return t


def build_swa(S_len):
    nc = _new_nc()
    NTL = S_len // 128
    t = _b_common(nc, S_len, 384, [("cos", [128, NTL, 32], F32), ("sin", [128, NTL, 32], F32),
                                    ("sink", [128, 512], F32), ("maskp", [128, 128], BF16)])
    with ExitStack() as es:
        C = Ctx(nc, es)
        S = C.S
        ident_b = C.sb([128, 128], BF16); maskc = C.sb([128, 128], BF16); maskp = C.sb([128, 128], BF16)
        cos = C.sb([128, NTL, 32], F32); sin = C.sb([128, NTL, 32], F32); esink = C.sb([128, 512], F32)
        W = C.sb([128, 8, 384], BF16)
        dcl = [Dep() for _ in range(7)]
        S.dma("sp", ident_b[:], t["ident_b"], W=[dcl[0]]); S.dma("sp", maskc[:], t["maskc"], W=[dcl[1]])
        S.dma("sp", maskp[:], t["maskp"], W=[dcl[2]]); S.dma("sp", cos[:], t["cos"], W=[dcl[3]]); S.dma("sp", sin[:], t["sin"], W=[dcl[4]])
        S.dma("sp", esink[:], t["sink"], W=[dcl[5]])
        S.dma("pool", W[:], t["w"].rearrange("(k p) f -> p k f", p=128), W=[dcl[6]])
        for e_ in S.eng.values():
            S.finish(e_["name"], dcl)
        dc = Dep()
        S.op("act", lambda e: e.activation(out=esink[:], in_=esink[:], func=AF.Exp), W=[dc])
        hblk = [C.sb([128, 8, 512], BF16) for _ in range(2)]; d_hblk = [Dep(), Dep()]
        pproj = [C.ps([128, 512], F32) for _ in range(2)]; d_pproj = [Dep(), Dep()]
        ptq = C.ps([128, 1024], BF16); d_ptq = Dep()
        pSc = C.ps([128, 512], F32); pSp = C.ps([128, 512], F32); d_pSc = Dep(); d_pSp = Dep()
        pO = [C.ps([128, 512], F32) for _ in range(2)]; d_pO = [Dep(), Dep()]
        qk32 = [C.sb([128, 5, 64], F32) for _ in range(2)]; d_qk32 = [Dep(), Dep()]
        tt_ = [C.sb([128, 5, 32], F32) for _ in range(4)]; d_tt = [Dep() for _ in range(4)]
        qkr = [C.sb([128, 8, 64], BF16) for _ in range(2)]; d_qkr = [Dep(), Dep()]
        qkT = [C.sb([128, 512], BF16) for _ in range(3)]; d_qkT = [Dep() for _ in range(3)]
        vaug = [C.sb([128, 128], BF16) for _ in range(3)]; d_vaug = [Dep() for _ in range(3)]
        ecur = [C.sb([128, 512], BF16) for _ in range(2)]; d_ecur = [Dep(), Dep()]
        eprv = [C.sb([128, 512], BF16) for _ in range(2)]; d_eprv = [Dep(), Dep()]
        pcur = [C.sb([128, 512], BF16) for _ in range(2)]; d_pcur = [Dep(), Dep()]
        pprv = [C.sb([128, 512], BF16) for _ in range(2)]; d_pprv = [Dep(), Dep()]
        den = [C.sb([128, 512], F32) for _ in range(2)]; d_den = [Dep(), Dep()]
        oTt = [C.sb([64, 4, 512], BF16) for _ in range(2)]; d_oTt = [Dep(), Dep()]
        for i in range(3):
            S.op("dve", lambda e, i=i: e.memset(vaug[i][:, 64:128], 1.0), W=[d_vaug[i]])
        for i in range(2):
            S.op("dve", lambda e, i=i: e.memset(qkr[i][:, 5:7, :], 0.0), W=[d_qkr[i]])
        outd = []
        for n in range(NTL):
            blk, tt = n // 4, n % 4
            hb, dhb = hblk[blk % 2], d_hblk[blk % 2]
            if tt == 0:
                _load_hT(S, hb, t["hT"], blk, S_len, dhb)
            pp, dpp = pproj[n % 2], d_pproj[n % 2]
            for k in range(8):
                S.op("pe", lambda e, k=k: e.matmul(pp[:, 0:384], hb[:, k, tt * 128:(tt + 1) * 128], W[:, k, :],
                                                   start=(k == 0), stop=(k == 7)), R=[dhb], W=[dpp], sig=(k == 7))
            q3, dq3 = qk32[n % 2], d_qk32[n % 2]
            va, dva = vaug[n % 3], d_vaug[n % 3]
            q3f = q3[:].rearrange("p h d -> p (h d)")
            S.op("act", lambda e: e.activation(out=q3f[:, 0:320], in_=pp[:, 0:320], func=AF.Identity), R=[dpp], W=[dq3])
            S.op("act", lambda e: e.activation(out=va[:, 0:64], in_=pp[:, 320:384], func=AF.Identity), R=[dpp], W=[dva])
            x1, x2 = q3[:, :, 0:32], q3[:, :, 32:64]
            cb = cos[:, n, :].unsqueeze(1).to_broadcast([128, 5, 32])
            sb_ = sin[:, n, :].unsqueeze(1).to_broadcast([128, 5, 32])
            qr, dqr = qkr[n % 2], d_qkr[n % 2]
            S.op("dve", lambda e: e.tensor_tensor(out=tt_[0][:], in0=x1, in1=cb, op=ALU.mult), R=[dq3], W=[d_tt[0]])
            S.op("pool", lambda e: e.tensor_tensor(out=tt_[1][:], in0=x2, in1=sb_, op=ALU.mult), R=[dq3], W=[d_tt[1]])
            S.op("dve", lambda e: e.tensor_tensor(out=qr[:, 0:5, 0:32], in0=tt_[0][:], in1=tt_[1][:], op=ALU.subtract),
                 R=[d_tt[0], d_tt[1]], W=[dqr])
            S.op("dve", lambda e: e.tensor_tensor(out=tt_[2][:], in0=x2, in1=cb, op=ALU.mult), R=[dq3], W=[d_tt[2]])
            S.op("pool", lambda e: e.tensor_tensor(out=tt_[3][:], in0=x1, in1=sb_, op=ALU.mult), R=[dq3], W=[d_tt[3]])
            S.op("dve", lambda e: e.tensor_tensor(out=qr[:, 0:5, 32:64], in0=tt_[2][:], in1=tt_[3][:], op=ALU.add),
                 R=[d_tt[2], d_tt[3]], W=[dqr])
            S.op("pool", lambda e: e.tensor_copy(out=qr[:, 7, :], in_=qr[:, 4, :]), R=[dqr], W=[dqr])
            qr2 = qr[:].rearrange("p h d -> p (h d)")
            for j in range(4):
                S.op("pe", lambda e, j=j: e.transpose(ptq[:, j * 128:(j + 1) * 128], qr2[:, j * 128:(j + 1) * 128], ident_b[:]),
                     R=[dqr], W=[d_ptq], sig=(j == 3))
            qT, dqT = qkT[n % 3], d_qkT[n % 3]
            S.op("act", lambda e: e.activation(out=qT[:], in_=ptq[:, 0:512], func=AF.Identity), R=[d_ptq], W=[dqT])
            for h in range(4):
                S.op("pe", lambda e, h=h: e.matmul(pSc[:, h * 128:(h + 1) * 128], qT[:, 256 + (h % 2) * 128:384 + (h % 2) * 128],
                                                   qT[:, (h // 2) * 128:(h // 2 + 1) * 128], start=True, stop=True),
                     R=[dqT], W=[d_pSc], sig=(h == 3))
            ec, dec_ = ecur[n % 2], d_ecur[n % 2]
            pc, dpc = pcur[n % 2], d_pcur[n % 2]
            S.op("act", lambda e: e.activation(out=ec[:], in_=pSc[:], func=AF.Exp, scale=0.125), R=[d_pSc], W=[dec_])
            S.op("dve", lambda e: e.tensor_tensor(out=pc[:].rearrange("p (h t) -> p h t", h=4), in0=ec[:].rearrange("p (h t) -> p h t", h=4),
                                                  in1=maskc[:].unsqueeze(1).to_broadcast([128, 4, 128]), op=ALU.mult),
                 R=[dec_], W=[dpc])
            po, dpo = pO[n % 2], d_pO[n % 2]
            if n > 0:
                qTp, dqTp = qkT[(n - 1) % 3], d_qkT[(n - 1) % 3]
                for h in range(4):
                    S.op("pe", lambda e, h=h: e.matmul(pSp[:, h * 128:(h + 1) * 128], qTp[:, 256 + (h % 2) * 128:384 + (h % 2) * 128],
                                                       qT[:, (h // 2) * 128:(h // 2 + 1) * 128], start=True, stop=True),
                         R=[dqT, dqTp], W=[d_pSp], sig=(h == 3))
                ep, dep_ = eprv[n % 2], d_eprv[n % 2]
                ppv, dppv = pprv[n % 2], d_pprv[n % 2]
                S.op("act", lambda e: e.activation(out=ep[:], in_=pSp[:], func=AF.Exp, scale=0.125), R=[d_pSp], W=[dep_])
                S.op("pool", lambda e: e.tensor_tensor(out=ppv[:].rearrange("p (h t) -> p h t", h=4), in0=ep[:].rearrange("p (h t) -> p h t", h=4),
                                                       in1=maskp[:].unsqueeze(1).to_broadcast([128, 4, 128]), op=ALU.mult),
                     R=[dep_], W=[dppv])
                vap, dvap = vaug[(n - 1) % 3], d_vaug[(n - 1) % 3]
                S.op("pe", lambda e: e.matmul(po[:], vap[:], ppv[:], start=True, stop=False), R=[dvap, dppv], W=[dpo], sig=False)
            S.op("pe", lambda e: e.matmul(po[:], va[:], pc[:], start=(n == 0), stop=True), R=[dva, dpc], W=[dpo])
            dn, ddn = den[n % 2], d_den[n % 2]
            ot, dot = oTt[blk % 2], d_oTt[blk % 2]
            S.op("dve", lambda e: e.tensor_tensor(out=dn[64:128, :], in0=po[64:128, :], in1=esink[64:128, :], op=ALU.add),
                 R=[dpo, dc], W=[ddn])
            S.op("dve", lambda e: e.reciprocal(out=dn[64:128, :], in_=dn[64:128, :]), R=[ddn], W=[ddn])
            S.op("dve", lambda e: e.tensor_tensor(
                out=ot[0:64, :, tt * 128:(tt + 1) * 128], in0=po[0:64, :].rearrange("p (h t) -> p h t", h=4),
                in1=dn[64:128, :].rearrange("p (h t) -> p h t", h=4), op=ALU.mult), R=[dpo, ddn], W=[dot])
            if tt == 3:
                S.dma("sp", _oT_dst(t["oT"], blk).rearrange("(h d) s -> d h s", d=64), ot[:], R=[dot])
                outd.append(dot)
        S.finish("sp", outd)
        S.barrier()
    return nc


GLA_STOP = 99


def build_gla(S_len):
    nc = _new_nc()
    NTL = S_len // 128
    t = _b_common(nc, S_len, 784, [("wg", [128, 128], F32), ("hn", [1, 256], F32),
                                    ("U_b", [128, 128], BF16), ("ones_b", [128, 128], BF16)])
    with ExitStack() as es:
        C = Ctx(nc, es)
        S = C.S
        ident_b = C.sb([128, 128], BF16); maskc = C.sb([128, 128], BF16)
        U_b = C.sb([128, 128], BF16); ones_b = C.sb([128, 128], BF16)
        wg = C.sb([128, 128], BF16); hn = C.sb([128, 256], F32)
        W = C.sb([128, 8, 784], BF16)
        dcl = [Dep() for _ in range(7)]
        S.dma("sp", ident_b[:], t["ident_b"], W=[dcl[0]]); S.dma("sp", maskc[:], t["maskc"], W=[dcl[1]])
        S.dma("sp", U_b[:], t["U_b"], W=[dcl[2]]); S.dma("sp", ones_b[:], t["ones_b"], W=[dcl[3]])
        S.dma("pool", wg[:], t["wg"], W=[dcl[4]]); S.dma("sp", hn[:], t["hn"].partition_broadcast(128), W=[dcl[5]])
        S.dma("pool", W[:], t["w"].rearrange("(k p) f -> p k f", p=128), W=[dcl[6]])
        for e_ in S.eng.values():
            S.finish(e_["name"], dcl)
        hblk = [C.sb([128, 8, 512], BF16) for _ in range(2)]; d_hblk = [Dep(), Dep()]
        pA = C.ps([128, 512], F32); pB = C.ps([128, 512], F32); ptr = C.ps([128, 1024], BF16); pZ = C.ps([128, 512], F32)
        pC = C.ps([128, 512], F32); pO = C.ps([128, 512], F32); pKV = C.ps([128, 512], F32); ptr2 = C.ps([128, 1024], BF16)
        dpA, dpB, dptr, dpZ, dpZ2, dpC, dpO, dpKV, dptr2 = [Dep() for _ in range(9)]
        a_tm = [C.sb([128, 128], BF16) for _ in range(2)]; d_atm = [Dep(), Dep()]
        aT = C.sb([128, 128], BF16); d_aT = Dep()
        e1 = C.sb([128, 128], F32); nla = C.sb([128, 128], F32); d_e1 = Dep(); d_nla = Dep()
        nhi = C.sb([128, 128], BF16); nlo = C.sb([128, 128], BF16); nr1 = C.sb([128, 128], F32)
        d_nhi, d_nlo, d_nr1 = Dep(), Dep(), Dep()
        eb = C.sb([128, 128], F32); enb = C.sb([128, 128], F32); etot = C.sb([128, 128], F32); ebl = C.sb([128, 128], F32)
        d_eb, d_enb, d_etot, d_ebl = Dep(), Dep(), Dep(), Dep()
        dcol = C.sb([128, 1], F32); d_dcol = Dep()
        qt = C.sb([128, 128], BF16); kt = C.sb([128, 128], BF16); d_qt = Dep(); d_kt = Dep()
        qk32 = C.sb([128, 256], F32); d_qk32 = Dep()
        kdec = C.sb([128, 128], BF16); d_kdec = Dep()
        v_tm = C.sb([128, 256], BF16); d_v = Dep()
        qkT = C.sb([128, 256], BF16); d_qkT = Dep()
        attnT = C.sb([128, 128], BF16); d_attnT = Dep()
        state = C.sb([128, 256], F32); state_b = C.sb([128, 256], BF16); d_state = Dep(); d_stb = Dep()
        s4 = C.sb([128, 4], F32); d_s4 = Dep()
        on = C.sb([128, 256], F32); gate = C.sb([128, 256], F32); og = C.sb([128, 256], BF16)
        d_on, d_gate, d_og = Dep(), Dep(), Dep()
        oTt = [C.sb([128, 2, 512], BF16) for _ in range(2)]; d_oTt = [Dep(), Dep()]
        S.op("dve", lambda e: e.memset(state[:], 0.0), W=[d_state])
        S.op("dve", lambda e: e.memset(state_b[:], 0.0), W=[d_stb])
        for i in range(2):
            S.op("dve", lambda e, i=i: e.memset(a_tm[i][:], 0.0), W=[d_atm[i]])
            S.op("dve", lambda e, i=i: e.memset(a_tm[i][:, 16:17], 1.0), W=[d_atm[i]])
        outd = []
        for n in range(NTL):
            blk, tt = n // 4, n % 4
            hb, dhb = hblk[blk % 2], d_hblk[blk % 2]
            if tt == 0:
                _load_hT(S, hb, t["hT"], blk, S_len, dhb)
            for k in range(8):
                S.op("pe", lambda e, k=k: e.matmul(pA[:, 0:512], hb[:, k, tt * 128:(tt + 1) * 128], W[:, k, 0:512],
                                                   start=(k == 0), stop=(k == 7)), R=[dhb], W=[dpA], sig=(k == 7))
            for k in range(8):
                S.op("pe", lambda e, k=k: e.matmul(pB[:, 0:272], hb[:, k, tt * 128:(tt + 1) * 128], W[:, k, 512:784],
                                                   start=(k == 0), stop=(k == 7)), R=[dhb], W=[dpB], sig=(k == 7))
            at, dat = a_tm[n % 2], d_atm[n % 2]
            S.op("act", lambda e: e.activation(out=at[:, 0:16], in_=pB[:, 256:272], func=AF.Identity), R=[dpB], W=[dat])
            S.op("pe", lambda e: e.transpose(ptr[:, 0:128], at[:], ident_b[:]), R=[dat], W=[dptr])
            S.op("act", lambda e: e.activation(out=aT[:], in_=ptr[:, 0:128], func=AF.Identity), R=[dptr], W=[d_aT])
            S.op("pe", lambda e: e.matmul(pZ[:, 0:128], aT[:], wg[:], start=True, stop=True), R=[d_aT], W=[dpZ])
            if GLA_STOP == 1:
                break
            S.op("act", lambda e: e.activation(out=e1[:], in_=pZ[:, 0:128], func=AF.Exp, scale=-1.0), R=[dpZ], W=[d_e1])
            S.op("act", lambda e: e.activation(out=nla[:], in_=e1[:], func=AF.Ln, bias=1.0), R=[d_e1], W=[d_nla])
            if GLA_STOP == 2:
                break
            S.op("act", lambda e: e.activation(out=nhi[:], in_=nla[:], func=AF.Identity), R=[d_nla], W=[d_nhi])
            S.op("dve", lambda e: e.tensor_tensor(out=nr1[:], in0=nla[:], in1=nhi[:], op=ALU.subtract), R=[d_nla, d_nhi], W=[d_nr1])
            S.op("dve", lambda e: e.tensor_scalar(out=nlo[:], in0=nr1[:], scalar1=1.0, scalar2=None, op0=ALU.mult), R=[d_nr1], W=[d_nlo])
            for i_, (src_, dsrc_) in enumerate(((nhi, d_nhi), (nlo, d_nlo))):
                S.op("pe", lambda e, src_=src_, i_=i_: e.matmul(pC[:, 0:128], U_b[:], src_[:], start=(i_ == 0), stop=(i_ == 1)),
                     R=[dsrc_], W=[dpC], sig=False)
            for i_, (src_, dsrc_) in enumerate(((nhi, d_nhi), (nlo, d_nlo))):
                S.op("pe", lambda e, src_=src_, i_=i_: e.matmul(pC[:, 128:256], ones_b[:], src_[:], start=(i_ == 0), stop=(i_ == 1)),
                     R=[dsrc_], W=[dpC], sig=False)
            for i_, (src_, dsrc_) in enumerate(((nhi, d_nhi), (nlo, d_nlo))):
                S.op("pe", lambda e, src_=src_, i_=i_: e.matmul(pC[:, 256:384], src_[:], ones_b[:], start=(i_ == 0), stop=(i_ == 1)),
                     R=[dsrc_], W=[dpC], sig=(i_ == 1))
            S.op("act", lambda e: e.activation(out=eb[:], in_=pC[:, 0:128], func=AF.Exp, scale=-1.0 / 16), R=[dpC], W=[d_eb])
            S.op("act", lambda e: e.activation(out=enb[:], in_=pC[:, 0:128], func=AF.Exp, scale=1.0 / 16), R=[dpC], W=[d_enb])
            S.op("act", lambda e: e.activation(out=etot[:], in_=pC[:, 128:256], func=AF.Exp, scale=-1.0 / 16), R=[dpC], W=[d_etot])
            S.op("act", lambda e: e.activation(out=dcol[:], in_=pC[:, 256:257], func=AF.Exp, scale=-1.0 / 16), R=[dpC], W=[d_dcol])
            if GLA_STOP == 3:
                break
            S.op("pool", lambda e: e.tensor_tensor(out=ebl[:], in0=enb[:], in1=etot[:], op=ALU.mult), R=[d_enb, d_etot], W=[d_ebl])
            S.op("act", lambda e: e.activation(out=qk32[:, 0:128], in_=pA[:, 0:128], func=AF.Identity, scale=128.0 ** -0.5), R=[dpA], W=[d_qk32])
            S.op("act", lambda e: e.activation(out=qk32[:, 128:256], in_=pA[:, 128:256], func=AF.Identity), R=[dpA], W=[d_qk32])
            S.op("dve", lambda e: e.tensor_tensor(out=qt[:], in0=qk32[:, 0:128], in1=eb[:], op=ALU.mult), R=[d_qk32, d_eb], W=[d_qt])
            S.op("dve", lambda e: e.tensor_tensor(out=kt[:], in0=qk32[:, 128:256], in1=enb[:], op=ALU.mult), R=[d_qk32, d_enb], W=[d_kt])
            S.op("dve", lambda e: e.tensor_tensor(out=kdec[:], in0=qk32[:, 128:256], in1=ebl[:], op=ALU.mult), R=[d_qk32, d_ebl], W=[d_kdec])
            S.op("act", lambda e: e.activation(out=v_tm[:], in_=pA[:, 256:512], func=AF.Identity), R=[dpA], W=[d_v])
            S.op("pe", lambda e: e.transpose(ptr[:, 128:256], qt[:], ident_b[:]), R=[d_qt], W=[dptr], sig=False)
            S.op("pe", lambda e: e.transpose(ptr[:, 256:384], kt[:], ident_b[:]), R=[d_kt], W=[dptr])
            S.op("act", lambda e: e.activation(out=qkT[:], in_=ptr[:, 128:384], func=AF.Identity), R=[dptr], W=[d_qkT])
            if GLA_STOP == 4:
                break
            S.op("pe", lambda e: e.matmul(pZ[:, 128:256], qkT[:, 128:256], qkT[:, 0:128], start=True, stop=True), R=[d_qkT], W=[dpZ2])
            S.op("dve", lambda e: e.tensor_tensor(out=attnT[:], in0=maskc[:], in1=pZ[:, 128:256], op=ALU.mult), R=[dpZ2], W=[d_attnT])
            S.op("pe", lambda e: e.matmul(pO[:, 0:256], attnT[:], v_tm[:], start=True, stop=False), R=[d_attnT, d_v], W=[dpO], sig=False)
            S.op("pe", lambda e: e.matmul(pO[:, 0:256], qkT[:, 0:128], state_b[:], start=False, stop=True), R=[d_qkT, d_stb], W=[dpO])
            S.op("pe", lambda e: e.matmul(pKV[:, 0:256], kdec[:], v_tm[:], start=True, stop=True), R=[d_kdec, d_v], W=[dpKV])
            if GLA_STOP == 5:
                break
            S.op("dve", lambda e: e.scalar_tensor_tensor(out=state[:], in0=state[:], scalar=dcol[:, 0:1], in1=pKV[:, 0:256],
                                                         op0=ALU.mult, op1=ALU.add), R=[d_dcol, dpKV, d_state], W=[d_state])
            S.op("act", lambda e: e.activation(out=state_b[:], in_=state[:], func=AF.Identity), R=[d_state], W=[d_stb])
            if GLA_STOP == 6:
                break
            S.op("pool", lambda e: e.memset(s4[:, 0:1], 0.0), W=[d_s4])
            S.op("act", lambda e: e.activation(out=on[:], in_=pO[:, 0:256], func=AF.Square, accum_out=s4[:, 0:1]), R=[dpO], W=[d_on, d_s4])
            S.op("act", lambda e: e.activation(out=s4[:, 1:2], in_=s4[:, 0:1], func=AF.Sqrt, scale=1.0 / 256, bias=EPS), R=[d_s4], W=[d_s4])
            S.op("dve", lambda e: e.reciprocal(out=s4[:, 2:3], in_=s4[:, 1:2]), R=[d_s4], W=[d_s4])
            S.op("dve", lambda e: e.scalar_tensor_tensor(out=on[:], in0=pO[:, 0:256], scalar=s4[:, 2:3], in1=hn[:],
                                                         op0=ALU.mult, op1=ALU.mult), R=[dpO, d_s4], W=[d_on])
            S.op("act", lambda e: e.activation(out=gate[:], in_=pB[:, 0:256], func=AF.Silu), R=[dpB], W=[d_gate])
            S.op("pool", lambda e: e.tensor_tensor(out=og[:], in0=on[:], in1=gate[:], op=ALU.mult), R=[d_on, d_gate], W=[d_og])
            if GLA_STOP == 7:
                break
            S.op("pe", lambda e: e.transpose(ptr2[:, 0:128], og[:, 0:128], ident_b[:]), R=[d_og], W=[dptr2], sig=False)
            S.op("pe", lambda e: e.transpose(ptr2[:, 128:256], og[:, 128:256], ident_b[:]), R=[d_og], W=[dptr2])
            ot, dot = oTt[blk % 2], d_oTt[blk % 2]
            for c in range(2):
                S.op("act", lambda e, c=c: e.activation(out=ot[:, c, tt * 128:(tt + 1) * 128], in_=ptr2[:, c * 128:(c + 1) * 128],
                                                         func=AF.Identity), R=[dptr2], W=[dot])
            if tt == 3:
                S.dma("sp", _oT_dst(t["oT"], blk).rearrange("(c p) s -> p c s", p=128), ot[:], R=[dot])
                outd.append(dot)
        S.finish("sp", outd)
        S.barrier()
    return nc


FOX_STOP = 99


def build_fox(S_len):
    nc = _new_nc()
    NTL = S_len // 128
    NG = S_len // 512
    NF = NTL * 4
    t = _b_common(nc, S_len, 772, [("bfb", [128, 4], F32), ("ones_b", [128, 128], BF16)])
    with ExitStack() as es:
        C = Ctx(nc, es)
        S = C.S
        ident_b = C.sb([128, 128], BF16); maskc = C.sb([128, 128], BF16); ones_b = C.sb([128, 128], BF16)
        bfb = C.sb([128, 4], F32)
        W = C.sb([128, 8, 772], BF16)
        dcl = [Dep() for _ in range(5)]
        S.dma("sp", ident_b[:], t["ident_b"], W=[dcl[0]]); S.dma("sp", maskc[:], t["maskc"], W=[dcl[1]])
        S.dma("sp", ones_b[:], t["ones_b"], W=[dcl[2]]); S.dma("sp", bfb[:], t["bfb"], W=[dcl[3]])
        S.dma("pool", W[:], t["w"].rearrange("(k p) f -> p k f", p=128), W=[dcl[4]])
        for e_ in S.eng.values():
            S.finish(e_["name"], dcl)
        hblk = [C.sb([128, 8, 512], BF16) for _ in range(2)]; d_hblk = [Dep(), Dep()]
        pproj = [C.ps([128, 512], F32) for _ in range(2)]; d_pproj = [Dep(), Dep()]
        ptr = C.ps([128, 1024], BF16); d_ptr = Dep()
        pS = [C.ps([128, 512], F32) for _ in range(3)]; d_pS = [Dep() for _ in range(3)]
        pO = [C.ps([128, 512], F32) for _ in range(2)]; d_pO = [Dep(), Dep()]
        zb = C.sb([128, NTL, 4], F32); d_zb = Dep()
        nlf = C.sb([128, NF], F32); d_nlf = Dep()
        r1 = C.sb([128, NF], F32); d_r1 = Dep()
        r2 = C.sb([128, NF], F32); d_r2 = Dep()
        hi = C.sb([128, NF], BF16); mid = C.sb([128, NF], BF16); lo = C.sb([128, NF], BF16)
        d_hi, d_mid, d_lo = Dep(), Dep(), Dep()
        cs = C.sb([128, NTL, 4], F32); tot = C.sb([128, NTL, 4], F32); carry = C.sb([128, NTL, 4], F32)
        d_cs, d_tot, d_carry = Dep(), Dep(), Dep()
        qaug = C.sb([128, NTL, 4, 6], BF16); kaug = C.sb([128, NTL, 4, 6], BF16); d_qaug = Dep(); d_kaug = Dep()
        qT = C.sb([128, S_len], BF16); kT = C.sb([128, S_len], BF16); V = C.sb([128, NTL, 128], BF16)
        d_qT, d_kT, d_V = Dep(), Dep(), Dep()
        q_tm = [C.sb([128, 128], BF16) for _ in range(2)]; d_qtm = [Dep(), Dep()]
        k_tm = [C.sb([128, 128], BF16) for _ in range(2)]; d_ktm = [Dep(), Dep()]
        Pt = [C.sb([128, 512], BF16) for _ in range(3)]; d_Pt = [Dep() for _ in range(3)]
        den = [C.sb([128, 512], F32) for _ in range(2)]; d_den = [Dep(), Dep()]
        oTt = [C.sb([64, 512], BF16) for _ in range(2)]; d_oTt = [Dep(), Dep()]
        dclamp = C.sb([128, 128], F32); d_dclamp = Dep()

        def split3(src, dsrc, h_, dh_, m_, dm_, l_, dl_):
            S.op("act", lambda e: e.activation(out=h_, in_=src, func=AF.Identity), R=[dsrc], W=[dh_])
            S.op("dve", lambda e: e.tensor_tensor(out=r1[:], in0=src, in1=h_, op=ALU.subtract), R=[dsrc, dh_], W=[d_r1])
            S.op("dve", lambda e: e.tensor_scalar(out=m_, in0=r1[:], scalar1=1.0, scalar2=None, op0=ALU.mult), R=[d_r1], W=[dm_])
            S.op("dve", lambda e: e.tensor_tensor(out=r2[:], in0=r1[:], in1=m_, op=ALU.subtract), R=[d_r1, dm_], W=[d_r2])
            S.op("dve", lambda e: e.tensor_scalar(out=l_, in0=r2[:], scalar1=1.0, scalar2=None, op0=ALU.mult), R=[d_r2], W=[dl_])

        for n in range(NTL):
            blk, tt = n // 4, n % 4
            hb, dhb = hblk[blk % 2], d_hblk[blk % 2]
            if tt == 0:
                _load_hT(S, hb, t["hT"], blk, S_len, dhb)
            pp, dpp = pproj[n % 2], d_pproj[n % 2]
            for k in range(8):
                S.op("pe", lambda e, k=k: e.matmul(pp[:, 0:128], hb[:, k, tt * 128:(tt + 1) * 128], W[:, k, 644:772],
                                                   start=(k == 0), stop=(k == 7)), R=[dhb], W=[dpp], sig=(k == 7))
            S.op("dve", lambda e: e.tensor_tensor(out=zb[:, n, :], in0=bfb[:], in1=pp[:, 124:128], op=ALU.add), R=[dpp], W=[d_zb])
        zbf = zb[:].rearrange("p n h -> p (n h)")
        if FOX_STOP == 1:
            S.barrier()
            return nc
        S.op("act", lambda e: e.activation(out=r1[:], in_=zbf, func=AF.Exp, scale=-1.0), R=[d_zb], W=[d_r1])
        S.op("act", lambda e: e.activation(out=nlf[:], in_=r1[:], func=AF.Ln, bias=1.0), R=[d_r1], W=[d_nlf])
        split3(nlf[:], d_nlf, hi[:], d_hi, mid[:], d_mid, lo[:], d_lo)
        for i_, (src_, dsrc_) in enumerate(((hi, d_hi), (mid, d_mid), (lo, d_lo))):
            S.op("pe", lambda e, src_=src_, i_=i_: e.matmul(pS[0][:, 0:NF], maskc[:], src_[:], start=(i_ == 0), stop=(i_ == 2)),
                 R=[dsrc_], W=[d_pS[0]], sig=(i_ == 2))
        for i_, (src_, dsrc_) in enumerate(((hi, d_hi), (mid, d_mid), (lo, d_lo))):
            S.op("pe", lambda e, src_=src_, i_=i_: e.matmul(pS[1][:, 0:NF], ones_b[:], src_[:], start=(i_ == 0), stop=(i_ == 2)),
                 R=[dsrc_], W=[d_pS[1]], sig=(i_ == 2))
        S.op("act", lambda e: e.activation(out=cs[:].rearrange("p n h -> p (n h)"), in_=pS[0][:, 0:NF], func=AF.Identity), R=[d_pS[0]], W=[d_cs])
        S.op("act", lambda e: e.activation(out=tot[:].rearrange("p n h -> p (n h)"), in_=pS[1][:, 0:NF], func=AF.Identity), R=[d_pS[1]], W=[d_tot])
        if FOX_STOP == 2:
            S.barrier()
            return nc
        S.op("dve", lambda e: e.memset(carry[:], 0.0), W=[d_carry])
        for j in range(1, NTL):
            S.op("dve", lambda e, j=j: e.tensor_tensor(out=carry[:, j, :], in0=carry[:, j - 1, :], in1=tot[:, j - 1, :], op=ALU.add),
                 R=[d_tot, d_carry], W=[d_carry])
        S.op("dve", lambda e: e.tensor_tensor(out=cs[:], in0=cs[:], in1=carry[:], op=ALU.add), R=[d_cs, d_carry], W=[d_cs])
        split3(cs[:].rearrange("p n h -> p (n h)"), d_cs, hi[:], d_hi, mid[:], d_mid, lo[:], d_lo)
        S.op("dve", lambda e: e.memset(qaug[:], 1.0), W=[d_qaug])
        S.op("dve", lambda e: e.memset(kaug[:], 1.0), W=[d_kaug])
        for i_, (src_, dsrc_) in enumerate(((hi, d_hi), (mid, d_mid), (lo, d_lo))):
            sv = src_[:].rearrange("p (n h) -> p n h", h=4)
            S.op("dve", lambda e, sv=sv, i_=i_: e.tensor_scalar(out=qaug[:, :, :, i_], in0=sv, scalar1=-1.0, scalar2=None, op0=ALU.mult),
                 R=[dsrc_], W=[d_qaug])
            S.op("pool", lambda e, sv=sv, i_=i_: e.tensor_copy(out=kaug[:, :, :, 3 + i_], in_=sv), R=[dsrc_], W=[d_kaug])
        for i in range(2):
            S.op("dve", lambda e, i=i: e.memset(q_tm[i][:], 0.0), W=[d_qtm[i]])
            S.op("dve", lambda e, i=i: e.memset(k_tm[i][:], 0.0), W=[d_ktm[i]])
        S.op("dve", lambda e: e.memset(V[:, :, 64:128], 1.0), W=[d_V])
        if FOX_STOP == 3:
            S.barrier()
            return nc
        outd = []
        gi = 0
        si = 0
        lb = NTL // 4
        for h in range(4):
            for n in range(NTL):
                blk, tt = n // 4, n % 4
                if tt == 0:
                    hb, dhb = hblk[lb % 2], d_hblk[lb % 2]
                    lb += 1
                    _load_hT(S, hb, t["hT"], blk, S_len, dhb)
                pp, dpp = pproj[n % 2], d_pproj[n % 2]
                for k in range(8):
                    S.op("pe", lambda e, k=k, hb=hb, pp=pp: e.matmul(pp[:, 0:192], hb[:, k, tt * 128:(tt + 1) * 128], W[:, k, h * 192:(h + 1) * 192],
                                                                     start=(k == 0), stop=(k == 7)), R=[dhb], W=[dpp], sig=(k == 7))
                qm, dqm = q_tm[n % 2], d_qtm[n % 2]
                km, dkm = k_tm[n % 2], d_ktm[n % 2]
                S.op("act", lambda e, qm=qm, pp=pp: e.activation(out=qm[:, 0:64], in_=pp[:, 0:64], func=AF.Identity, scale=0.125), R=[dpp], W=[dqm])
                S.op("act", lambda e, km=km, pp=pp: e.activation(out=km[:, 0:64], in_=pp[:, 64:128], func=AF.Identity), R=[dpp], W=[dkm])
                S.op("act", lambda e, pp=pp, n=n: e.activation(out=V[:, n, 0:64], in_=pp[:, 128:192], func=AF.Identity), R=[dpp], W=[d_V])
                S.op("pool", lambda e, qm=qm, n=n: e.tensor_copy(out=qm[:, 64:70], in_=qaug[:, n, h, :]), R=[d_qaug], W=[dqm])
                S.op("pool", lambda e, km=km, n=n: e.tensor_copy(out=km[:, 64:70], in_=kaug[:, n, h, :]), R=[d_kaug], W=[dkm])
                S.op("pe", lambda e, qm=qm: e.transpose(ptr[:, 0:128], qm[:], ident_b[:]), R=[dqm], W=[d_ptr], sig=False)
                S.op("pe", lambda e, km=km: e.transpose(ptr[:, 128:256], km[:], ident_b[:]), R=[dkm], W=[d_ptr])
                S.op("act", lambda e, n=n: e.activation(out=qT[:, n * 128:(n + 1) * 128], in_=ptr[:, 0:128], func=AF.Identity), R=[d_ptr], W=[d_qT])
                S.op("act", lambda e, n=n: e.activation(out=kT[:, n * 128:(n + 1) * 128], in_=ptr[:, 128:256], func=AF.Identity), R=[d_ptr], W=[d_kT])
            if FOX_STOP == 4:
                S.barrier()
                return nc
            for m in range(NG):
                po, dpo = pO[gi % 2], d_pO[gi % 2]
                nj = 4 * m + 4
                for j in range(nj):
                    r = max(0, j - 4 * m)
                    c0 = r * 128
                    ps_, dps_ = pS[si % 3], d_pS[si % 3]
                    P_, dP_ = Pt[si % 3], d_Pt[si % 3]
                    si += 1
                    S.op("pe", lambda e, ps_=ps_, j=j, c0=c0, m=m: e.matmul(ps_[:, c0:512], kT[:, j * 128:(j + 1) * 128],
                                                                            qT[:, m * 512 + c0:(m + 1) * 512], start=True, stop=True),
                         R=[d_qT, d_kT], W=[dps_])
                    if j >= 4 * m:
                        S.op("dve", lambda e, ps_=ps_, c0=c0: e.tensor_scalar(out=dclamp[:], in0=ps_[:, c0:c0 + 128], scalar1=60.0, scalar2=None,
                                                                              op0=ALU.min), R=[dps_], W=[d_dclamp])
                        S.op("act", lambda e, P_=P_, c0=c0: e.activation(out=P_[:, c0:c0 + 128], in_=dclamp[:], func=AF.Exp), R=[d_dclamp], W=[dP_])
                        if c0 + 128 < 512:
                            S.op("act", lambda e, ps_=ps_, P_=P_, c0=c0: e.activation(out=P_[:, c0 + 128:512], in_=ps_[:, c0 + 128:512], func=AF.Exp),
                                 R=[dps_], W=[dP_])
                        S.op("pool", lambda e, P_=P_, c0=c0: e.tensor_tensor(out=P_[:, c0:c0 + 128], in0=P_[:, c0:c0 + 128], in1=maskc[:], op=ALU.mult),
                             R=[dP_], W=[dP_])
                    else:
                        S.op("act", lambda e, ps_=ps_, P_=P_, c0=c0: e.activation(out=P_[:, c0:512], in_=ps_[:, c0:512], func=AF.Exp), R=[dps_], W=[dP_])
                    S.op("pe", lambda e, po=po, P_=P_, j=j, c0=c0, nj=nj: e.matmul(po[:, c0:512], V[:, j, :], P_[:, c0:512],
                                                                                 start=(j == 0), stop=(j == nj - 1), skip_group_check=True),
                         R=[d_V, dP_], W=[dpo], sig=(j == nj - 1))
                dn, ddn = den[gi % 2], d_den[gi % 2]
                ot, dot = oTt[gi % 2], d_oTt[gi % 2]
                gi += 1
                S.op("dve", lambda e, dn=dn, po=po: e.reciprocal(out=dn[64:128, :], in_=po[64:128, :]), R=[dpo], W=[ddn])
                S.op("dve", lambda e, dn=dn, po=po, ot=ot: e.tensor_tensor(out=ot[0:64, :], in0=dn[64:128, :], in1=po[0:64, :], op=ALU.mult),
                     R=[dpo, ddn], W=[dot])
                S.dma("sp", _oT_dst(t["oT"], m)[h * 64:(h + 1) * 64, :], ot[:], R=[dot])
                outd.append(dot)
                if FOX_STOP == 5:
                    S.barrier()
                    return nc
        S.finish("sp", outd)
        S.barrier()
    return nc


_CONST = {}


def consts():
    if not _CONST:
        i = np.arange(128)
        mc = (i[:, None] <= i[None, :]).astype(np.float32)
        _CONST.update(ident_b=np.eye(128, dtype=np.float32).astype(NPBF), ident_f=np.eye(128, dtype=np.float32),
                      maskc=mc.astype(NPBF), maskp=(1.0 - mc).astype(NPBF), U_f=mc, ones_f=np.ones((128, 128), np.float32),
                      ones_b=np.ones((128, 128), np.float32).astype(NPBF))
    return _CONST


def rope_tables(S_len):
    key = ("rope", S_len)
    if key not in _CONST:
        hd = 64
        inv = (1.0 / (np.float32(150000.0) ** (np.arange(0, hd, 2, dtype=np.float32) / np.float32(hd)))).astype(np.float32)
        pos = np.arange(S_len, dtype=np.float32)
        ang = (pos[:, None] * inv[None, :]).astype(np.float32)
        cos = np.cos(ang).astype(np.float32).reshape(S_len // 128, 128, 32).transpose(1, 0, 2)
        sin = np.sin(ang).astype(np.float32).reshape(S_len // 128, 128, 32).transpose(1, 0, 2)
        _CONST[key] = (np.ascontiguousarray(cos), np.ascontiguousarray(sin))
    return _CONST[key]


def prep_swa(hT, w_in, sinks, g, S_len):
    c = consts()
    cos, sin = rope_tables(S_len)
    w = np.concatenate([w_in[:, g * 256:(g + 1) * 256], w_in[:, 1024 + g * 64:1024 + (g + 1) * 64],
                        w_in[:, 1280 + g * 64:1280 + (g + 1) * 64]], axis=1)
    sk = sinks[g * 4:(g + 1) * 4]
    sink = np.ascontiguousarray(np.broadcast_to(np.repeat(sk, 128)[None, :], (128, 512))).astype(np.float32)
    return dict(b_hT=hT, b_w=np.ascontiguousarray(w), b_ident_b=c["ident_b"], b_maskc=c["maskc"], b_maskp=c["maskp"],
                b_cos=cos, b_sin=sin, b_sink=sink)


def prep_gla(hT, w_in, w_gate_up, b_gate, head_norm, g, S_len):
    c = consts()
    w = np.concatenate([w_in[:, g * 128:(g + 1) * 128], w_in[:, 512 + g * 128:512 + (g + 1) * 128],
                        w_in[:, 1024 + g * 256:1024 + (g + 1) * 256], w_in[:, 2048 + g * 256:2048 + (g + 1) * 256],
                        w_in[:, 3072:3088]], axis=1)
    wg = np.zeros((128, 128), np.float32)
    wg[0:16] = w_gate_up[:, g * 128:(g + 1) * 128]
    wg[16] = b_gate[g * 128:(g + 1) * 128]
    return dict(b_hT=hT, b_w=np.ascontiguousarray(w), b_ident_b=c["ident_b"], b_maskc=c["maskc"], b_wg=wg,
                b_hn=np.ascontiguousarray(head_norm.reshape(1, 256)).astype(np.float32), b_U_b=c["maskc"], b_ones_b=c["ones_b"])


def prep_fox(hT, w_in, b_f, g, S_len):
    c = consts()
    cols = []
    for h in range(4 * g, 4 * g + 4):
        cols += [w_in[:, h * 64:(h + 1) * 64], w_in[:, 1024 + h * 64:1024 + (h + 1) * 64], w_in[:, 2048 + h * 64:2048 + (h + 1) * 64]]
    cols.append(w_in[:, 3072 + 4 * g:3072 + 4 * g + 4])
    w = np.concatenate(cols, axis=1)
    bfb = np.ascontiguousarray(np.broadcast_to(b_f[4 * g:4 * g + 4][None, :], (128, 4))).astype(np.float32)
    return dict(b_hT=hT, b_w=np.ascontiguousarray(w), b_ident_b=c["ident_b"], b_maskc=c["maskc"], b_bfb=bfb, b_ones_b=c["ones_b"])


_PROG = {}


def _prog(key, fn):
    if key not in _PROG:
        _PROG[key] = fn()
    return _PROG[key]


def _launch(nc, in_maps):
    res = run_bass_kernel_spmd(nc, in_maps, core_ids=list(range(8)))
    return res.results


GROUPS = [[0, 1, 2, 3], [4, 5, 6, 7]]
I32 = mybir.dt.int32
_BF = {0: 384, 1: 784, 2: 772}


def build_fused(S_len, depth=4):
    nc = bass.Bass("TRN2", target_bir_lowering=False)
    T = S_len // 4
    NTL = S_len // 128
    IN = "ExternalInput"
    ext = lambda n, sh, dt: nc.dram_tensor(n, list(sh), dt, kind=IN)
    x_ext = ext("x", [T, D], F32); c_col = ext("c_col", [128, 8], F32); rk = ext("rk", [1, 1], I32)
    cn = dict(ident_b=ext("ident_b", [128, 128], BF16), ident_f=ext("ident_f", [128, 128], F32),
              maskc=ext("maskc", [128, 128], BF16), maskp=ext("maskp", [128, 128], BF16), ones_b=ext("ones_b", [128, 128], BF16),
              cos=ext("cos", [128, NTL, 32], F32), sin=ext("sin", [128, NTL, 32], F32))
    gainf = ext("gainf", [1, D], F32)
    out_ext = nc.dram_tensor("out", [T, D], F32, kind="ExternalOutput")
    L = []
    for li in range(depth):
        kind = li % 3
        d = dict(ada_w=ext("ada_w%d" % li, [D, 6 * D], F32), ada_b=ext("ada_b%d" % li, [1, 6 * D], F32),
                 gain1=ext("gain1_%d" % li, [1, D], F32), gain2=ext("gain2_%d" % li, [1, D], F32),
                 w_gu=ext("w_gu%d" % li, [D, 2 * DFF], F32), w_dn=ext("w_dn%d" % li, [DFF, D], F32), w_o=ext("w_o%d" % li, [D, D], F32),
                 bw=ext("bw%d" % li, [D, _BF[kind]], F32))
        if kind == 0:
            d["sink"] = ext("sink%d" % li, [128, 512], F32)
        elif kind == 1:
            d["wg"] = ext("wg%d" % li, [128, 128], F32); d["hn"] = ext("hn%d" % li, [1, 256], F32)
        else:
            d["bfb"] = ext("bfb%d" % li, [128, 4], F32)
        d["hT_loc"] = nc.dram_tensor("hT_loc%d" % li, [D, T], BF16)
        d["hT_all"] = nc.dram_tensor("hT_all%d" % li, [4, 4 * 256, T], BF16)
        d["oT_loc"] = nc.dram_tensor("oT_loc%d" % li, [4, 256, T], BF16)
        d["oT_all"] = nc.dram_tensor("oT_all%d" % li, [4, D, T], BF16)
        L.append(d)
    x_buf = nc.dram_tensor("x_buf", [T, D], F32)
    with ExitStack() as es:
        S = Sched(nc, es)
        rank_reg = es.enter_context(nc.sync.register("rank"))
        nc.sync.reg_load(rank_reg, rk.ap()[0:1, 0:1])
        rank = nc.sync.snap(rank_reg, min_val=0, max_val=3)
        wgu_p = es.enter_context(nc.sbuf_tensor("wgu_persist", [128, 8, 2 * DFF], BF16))
        d_wgu_p = Dep()
        _FUSED.update(nc=nc, S=S, rank=rank, T=T, wgu=(wgu_p, d_wgu_p))
        try:
            for li in range(depth + 1):
                has_prev, final = li > 0, li == depth
                ov = dict(x_in=(x_ext.ap() if li <= 1 else x_buf.ap()), c_col=c_col.ap(), ident_b=cn["ident_b"].ap(), ident_f=cn["ident_f"].ap())
                if has_prev:
                    p = L[li - 1]
                    ov.update(oT=p["oT_all"].ap(), w_o=p["w_o"].ap(), w_gu=p["w_gu"].ap(), w_dn=p["w_dn"].ap(),
                              ada_wp=p["ada_w"].ap()[:, 2 * D:6 * D], ada_bp=p["ada_b"].ap()[:, 2 * D:6 * D], gain2=p["gain2"].ap())
                if not final:
                    q = L[li]
                    ov.update(ada_wc=q["ada_w"].ap()[:, 0:2 * D], ada_bc=q["ada_b"].ap()[:, 0:2 * D], gain1=q["gain1"].ap(),
                              hT_out=q["hT_loc"].ap(), x_out=x_buf.ap())
                else:
                    ov.update(gainf=gainf.ap(), out=out_ext.ap())
                _FUSED["ov"] = ov
                build_A(T, has_prev, final)
                if final:
                    break
                q = L[li]
                wv = q["w_gu"].ap().rearrange("(k p) n -> p k n", p=128)
                for jj in range(11):
                    S.dma_prefetch(wgu_p[:, :, jj * 512:(jj + 1) * 512], wv[:, :, jj * 512:(jj + 1) * 512], W=[d_wgu_p], acc=(jj > 0))
                for kk in range(4):
                    S.op("pool", lambda e, q=q, kk=kk: e.collective_compute(
                        "AllGather", ALU.bypass, replica_groups=GROUPS,
                        ins=[q["hT_loc"].ap()[kk * 256:(kk + 1) * 256, :].opt()], outs=[q["hT_all"].ap()[kk].opt()]))
                S.barrier()
                kind = li % 3
                ov = dict(b_hT=q["hT_all"].ap(), b_w=q["bw"].ap(), b_ident_b=cn["ident_b"].ap(), b_maskc=cn["maskc"].ap(), oT_loc=q["oT_loc"].ap())
                if kind == 0:
                    ov.update(b_cos=cn["cos"].ap(), b_sin=cn["sin"].ap(), b_sink=q["sink"].ap(), b_maskp=cn["maskp"].ap())
                    _FUSED["ov"] = ov
                    build_swa(S_len)
                elif kind == 1:
                    ov.update(b_wg=q["wg"].ap(), b_hn=q["hn"].ap(), b_U_b=cn["maskc"].ap(), b_ones_b=cn["ones_b"].ap())
                    _FUSED["ov"] = ov
                    build_gla(S_len)
                else:
                    ov.update(b_bfb=q["bfb"].ap(), b_ones_b=cn["ones_b"].ap())
                    _FUSED["ov"] = ov
                    build_fox(S_len)
                for qq in range(4):
                    S.op("pool", lambda e, q=q, qq=qq: e.collective_compute(
                        "AllGather", ALU.bypass, replica_groups=GROUPS,
                        ins=[q["oT_loc"].ap()[qq].opt()], outs=[q["oT_all"].ap()[qq].opt()]))
                S.barrier()
        finally:
            _FUSED.update(nc=None, S=None, ov=None, rank=None, T=None, wgu=None)
    return nc


def kernel(x, c, ada_w, ada_b, norm_gain, ffn_w_gu, ffn_w_down, swa_w_in, swa_sinks, swa_w_o,
           gla_w_in, gla_w_gate_up, gla_b_gate, gla_head_norm, gla_w_o, fox_w_in, fox_b_f, fox_w_o, final_norm):
    f32 = lambda a: np.ascontiguousarray(np.asarray(a, dtype=np.float32))
    x = f32(x); c = f32(c); ada_w = f32(ada_w); ada_b = f32(ada_b); norm_gain = f32(norm_gain)
    ffn_w_gu = f32(ffn_w_gu); ffn_w_down = f32(ffn_w_down)
    B, S_len, _ = x.shape
    T = S_len // 4
    depth = ada_w.shape[0]
    cst = consts()
    cos, sin = rope_tables(S_len)
    nc = _prog(("fused", S_len, depth), lambda: build_fused(S_len, depth))
    shared = dict(ident_b=cst["ident_b"], ident_f=cst["ident_f"], maskc=cst["maskc"], maskp=cst["maskp"], ones_b=cst["ones_b"],
                  cos=cos, sin=sin, gainf=f32(final_norm)[None, :])
    for li in range(depth):
        kind, j = li % 3, li // 3
        shared.update({"ada_w%d" % li: ada_w[li], "ada_b%d" % li: np.ascontiguousarray(ada_b[li][None, :]),
                       "gain1_%d" % li: np.ascontiguousarray(norm_gain[li, 0][None, :]), "gain2_%d" % li: np.ascontiguousarray(norm_gain[li, 1][None, :]),
                       "w_gu%d" % li: ffn_w_gu[li], "w_dn%d" % li: ffn_w_down[li], "w_o%d" % li: f32((swa_w_o, gla_w_o, fox_w_o)[kind][j])})
    maps = []
    dummy = np.zeros((1024, 8), NPBF)
    for i in range(8):
        b, g = i // 4, i % 4
        m = dict(shared)
        m.update(x=np.ascontiguousarray(x[b, g * T:(g + 1) * T]), c_col=np.ascontiguousarray(c[b].reshape(8, 128).T), rk=np.array([[g]], np.int32))
        for li in range(depth):
            kind, j = li % 3, li // 3
            if kind == 0:
                pm = prep_swa(dummy, f32(swa_w_in[j]), f32(swa_sinks[j]), g, S_len)
                m.update({"bw%d" % li: pm["b_w"], "sink%d" % li: pm["b_sink"]})
            elif kind == 1:
                pm = prep_gla(dummy, f32(gla_w_in[j]), f32(gla_w_gate_up[j]), f32(gla_b_gate[j]), f32(gla_head_norm[j]), g, S_len)
                m.update({"bw%d" % li: pm["b_w"], "wg%d" % li: pm["b_wg"], "hn%d" % li: pm["b_hn"]})
            else:
                pm = prep_fox(dummy, f32(fox_w_in[j]), f32(fox_b_f[j]), g, S_len)
                m.update({"bw%d" % li: pm["b_w"], "bfb%d" % li: pm["b_bfb"]})
        maps.append(m)
    res = run_bass_kernel_spmd(nc, maps, core_ids=list(range(8))).results
    out = np.zeros((B, S_len, D), np.float32)
    for i in range(8):
        b, g = i // 4, i % 4
        out[b, g * T:(g + 1) * T] = np.asarray(res[i]["out"], dtype=np.float32)
    return out


def kernel_unfused(x, c, ada_w, ada_b, norm_gain, ffn_w_gu, ffn_w_down, swa_w_in, swa_sinks, swa_w_o,
           gla_w_in, gla_w_gate_up, gla_b_gate, gla_head_norm, gla_w_o, fox_w_in, fox_b_f, fox_w_o, final_norm):
    f32 = lambda a: np.ascontiguousarray(np.asarray(a, dtype=np.float32))
    x = f32(x); c = f32(c); ada_w = f32(ada_w); ada_b = f32(ada_b); norm_gain = f32(norm_gain)
    ffn_w_gu = f32(ffn_w_gu); ffn_w_down = f32(ffn_w_down)
    B, S_len, _ = x.shape
    T = S_len // 4
    depth = ada_w.shape[0]
    cst = consts()
    mix_wo = []
    for i in range(depth):
        kind, j = i % 3, i // 3
        mix_wo.append(f32((swa_w_o, gla_w_o, fox_w_o)[kind][j]))
    ccol = [np.ascontiguousarray(c[b].reshape(8, 128).T) for b in range(B)]
    xcur = [[np.ascontiguousarray(x[b, r * T:(r + 1) * T]) for r in range(4)] for b in range(B)]
    oT_full = None
    out = np.zeros((B, S_len, D), np.float32)
    for li in range(depth + 1):
        has_prev, final = li > 0, li == depth
        nc = _prog(("A", T, has_prev, final), lambda: build_A(T, has_prev, final))
        maps = []
        for i in range(8):
            b, r = i // 4, i % 4
            m = dict(x_in=xcur[b][r], c_col=ccol[b], ident_b=cst["ident_b"], ident_f=cst["ident_f"])
            if has_prev:
                p = li - 1
                m.update(oT=np.ascontiguousarray(oT_full[b][:, r * T:(r + 1) * T]), w_o=mix_wo[p], w_gu=f32(ffn_w_gu[p]),
                         w_dn=f32(ffn_w_down[p]), ada_wp=np.ascontiguousarray(ada_w[p][:, 2 * D:6 * D]),
                         ada_bp=np.ascontiguousarray(ada_b[p][None, 2 * D:6 * D]), gain2=np.ascontiguousarray(norm_gain[p, 1][None, :]))
            if not final:
                m.update(ada_wc=np.ascontiguousarray(ada_w[li][:, 0:2 * D]), ada_bc=np.ascontiguousarray(ada_b[li][None, 0:2 * D]),
                         gain1=np.ascontiguousarray(norm_gain[li, 0][None, :]))
            else:
                m.update(gainf=f32(final_norm)[None, :])
            maps.append(m)
        res = _launch(nc, maps)
        if final:
            for i in range(8):
                b, r = i // 4, i % 4
                out[b, r * T:(r + 1) * T] = np.asarray(res[i]["out"], dtype=np.float32)
            break
        if has_prev:
            xcur = [[np.asarray(res[b * 4 + r]["x_out"]) for r in range(4)] for b in range(B)]
        hT_full = [np.ascontiguousarray(np.concatenate([np.asarray(res[b * 4 + r]["hT_out"]) for r in range(4)], axis=1)) for b in range(B)]
        kind, j = li % 3, li // 3
        if kind == 0:
            ncb = _prog(("swa", S_len), lambda: build_swa(S_len))
            maps = [prep_swa(hT_full[i // 4], f32(swa_w_in[j]), f32(swa_sinks[j]), i % 4, S_len) for i in range(8)]
        elif kind == 1:
            ncb = _prog(("gla", S_len), lambda: build_gla(S_len))
            maps = [prep_gla(hT_full[i // 4], f32(gla_w_in[j]), f32(gla_w_gate_up[j]), f32(gla_b_gate[j]), f32(gla_head_norm[j]),
                             i % 4, S_len) for i in range(8)]
        else:
            ncb = _prog(("fox", S_len), lambda: build_fox(S_len))
            maps = [prep_fox(hT_full[i // 4], f32(fox_w_in[j]), f32(fox_b_f[j]), i % 4, S_len) for i in range(8)]
        resb = _launch(ncb, maps)
        oT_full = [np.ascontiguousarray(np.concatenate([np.asarray(resb[b * 4 + g]["oT_loc"]) for g in range(4)], axis=0)) for b in range(B)]
    return out
```
